# Optimizing a Trainium2 kernel written in Bass

```python
import math
import jax, jax.numpy as jnp
from jax import lax
import numpy as np

D_MODEL = 1024
BATCH = 16
SEQ = 2048
DEPTH = 4

CHUNK = 64
D_MIX = D_MODEL
LRU_HEADS = 8
LRU_WIDTH = D_MIX // 2
LRU_HEAD_DIM = LRU_WIDTH // LRU_HEADS
SC_GROUPS = 8
SC_WIDTH = D_MIX - LRU_WIDTH
SC_GROUP_DIM = SC_WIDTH // SC_GROUPS
LRU_CONV = 4
SC_CONV = 3
MLP_CONV = 3
D_FF = 3 * D_MODEL
LRU_C = 8.0
IN_COLS = 2 * LRU_WIDTH + 3 * SC_WIDTH
EPS = 1e-6

kernel_name = "hybrid_rglru_shortconv_convffn_trunk"


def rms_norm(x, g):
    xf = x.astype(jnp.float32)
    y = xf * lax.rsqrt(jnp.mean(xf * xf, axis=-1, keepdims=True) + EPS)
    return (y * g.astype(jnp.float32)).astype(x.dtype)


def causal_dwconv(x, w, b=None):
    k_width = w.shape[0]
    s = x.shape[1]
    xp = jnp.pad(x, ((0, 0), (k_width - 1, 0), (0, 0)))
    y = xp[:, 0:s] * w[0]
    for k in range(1, k_width):
        y = y + xp[:, k:k + s] * w[k]
    if b is not None:
        y = y + b
    return y


def block_diag_linear(x, w, b):
    bsz, s, _ = x.shape
    h, dh, _ = w.shape
    xh = x.reshape(bsz, s, h, dh)
    y = jnp.einsum('bshd,hde->bshe', xh, w) + b
    return y.reshape(bsz, s, h * dh)


def linear_scan(a, b):
    def combine(left, right):
        a_l, b_l = left
        a_r, b_r = right
        return a_l * a_r, a_r * b_l + b_r
    _, h = lax.associative_scan(combine, (a, b), axis=1)
    return h


def rg_lru(x, w_a, b_a, w_x, b_x, lam):
    xf = x.astype(jnp.float32)
    r = jax.nn.sigmoid(block_diag_linear(xf, w_a.astype(jnp.float32), b_a.astype(jnp.float32)))
    i = jax.nn.sigmoid(block_diag_linear(xf, w_x.astype(jnp.float32), b_x.astype(jnp.float32)))
    log_a = -LRU_C * r * jax.nn.softplus(-lam.astype(jnp.float32))
    a = jnp.exp(log_a)
    mult = jnp.sqrt(jnp.clip(-jnp.expm1(2.0 * log_a), 0.0, None))
    h = linear_scan(a, mult * (i * xf))
    return h.astype(x.dtype)


def setup_inputs(seed: int = 0) -> dict:
    key = jax.random.key(seed)
    ks = jax.random.split(key, 24)
    f32 = jnp.float32

    def nrm(k, shape, fan_in):
        return jax.random.normal(k, shape, f32) * (fan_in ** -0.5)

    def gain(k, shape):
        return 1.0 + 0.05 * jax.random.normal(k, shape, f32)

    def bias(k, shape):
        return 0.01 * jax.random.normal(k, shape, f32)

    u = jax.random.uniform(ks[7], (DEPTH, LRU_WIDTH), f32, 0.9, 0.999)
    s = u ** (1.0 / LRU_C)
    lam = jnp.log(s) - jnp.log1p(-s)

    return {
        "x": jax.random.normal(ks[0], (BATCH, SEQ, D_MODEL), f32),
        "norm1_g": gain(ks[1], (DEPTH, D_MODEL)),
        "w_in": nrm(ks[2], (DEPTH, D_MODEL, IN_COLS), D_MODEL),
        "lru_conv_w": nrm(ks[3], (DEPTH, LRU_CONV, LRU_WIDTH), LRU_CONV),
        "lru_conv_b": bias(ks[4], (DEPTH, LRU_WIDTH)),
        "lru_w_a": nrm(ks[5], (DEPTH, LRU_HEADS, LRU_HEAD_DIM, LRU_HEAD_DIM), LRU_HEAD_DIM),
        "lru_b_a": bias(ks[6], (DEPTH, LRU_HEADS, LRU_HEAD_DIM)),
        "lru_w_x": nrm(ks[8], (DEPTH, LRU_HEADS, LRU_HEAD_DIM, LRU_HEAD_DIM), LRU_HEAD_DIM),
        "lru_b_x": bias(ks[9], (DEPTH, LRU_HEADS, LRU_HEAD_DIM)),
        "lru_lambda": lam,
        "sc_conv_w": nrm(ks[10], (DEPTH, SC_CONV, SC_WIDTH), SC_CONV),
        "lru_out_g": gain(ks[11], (DEPTH, LRU_WIDTH)),
        "sc_out_g": gain(ks[12], (DEPTH, SC_WIDTH)),
        "w_out": nrm(ks[13], (DEPTH, D_MIX, D_MODEL), D_MIX),
        "norm2_g": gain(ks[14], (DEPTH, D_MODEL)),
        "mlp_w_up": nrm(ks[15], (DEPTH, D_MODEL, D_FF), D_MODEL),
        "mlp_w_gate": nrm(ks[16], (DEPTH, D_MODEL, D_FF), D_MODEL),
        "mlp_conv_w": nrm(ks[17], (DEPTH, MLP_CONV, D_FF), MLP_CONV),
        "mlp_conv_b": bias(ks[18], (DEPTH, D_FF)),
        "mlp_w_down": nrm(ks[19], (DEPTH, D_FF, D_MODEL), D_FF),
        "final_g": gain(ks[20], (D_MODEL,)),
    }


def reference(x, norm1_g, w_in, lru_conv_w, lru_conv_b, lru_w_a, lru_b_a, lru_w_x, lru_b_x,
              lru_lambda, sc_conv_w, lru_out_g, sc_out_g, w_out, norm2_g, mlp_w_up, mlp_w_gate,
              mlp_conv_w, mlp_conv_b, mlp_w_down, final_g):
    h = x
    for l in range(DEPTH):
        u = rms_norm(h, norm1_g[l])
        proj = jnp.einsum('bsd,dc->bsc', u, w_in[l])
        o = 0
        lru_x = proj[..., o:o + LRU_WIDTH]; o += LRU_WIDTH
        lru_gate = proj[..., o:o + LRU_WIDTH]; o += LRU_WIDTH
        sc_b = proj[..., o:o + SC_WIDTH]; o += SC_WIDTH
        sc_c = proj[..., o:o + SC_WIDTH]; o += SC_WIDTH
        sc_v = proj[..., o:o + SC_WIDTH]

        xa = causal_dwconv(lru_x, lru_conv_w[l], lru_conv_b[l])
        ya = rg_lru(xa, lru_w_a[l], lru_b_a[l], lru_w_x[l], lru_b_x[l], lru_lambda[l])
        ya = ya * jax.nn.gelu(lru_gate, approximate=True)

        yb = sc_b * causal_dwconv(sc_c * sc_v, sc_conv_w[l])

        y = jnp.concatenate([rms_norm(ya, lru_out_g[l]), rms_norm(yb, sc_out_g[l])], axis=-1)
        h = h + jnp.einsum('bsc,cd->bsd', y, w_out[l])

        v = rms_norm(h, norm2_g[l])
        up = jnp.einsum('bsd,df->bsf', v, mlp_w_up[l])
        gate = jnp.einsum('bsd,df->bsf', v, mlp_w_gate[l])
        act = jax.nn.gelu(causal_dwconv(up, mlp_conv_w[l], mlp_conv_b[l]), approximate=True) * gate
        h = h + jnp.einsum('bsf,fd->bsd', act, mlp_w_down[l])
    return rms_norm(h, final_g)
```

```python
import numpy as np
from contextlib import ExitStack
import concourse.bass as bass
import concourse.mybir as mybir
from concourse.bass_utils import run_bass_kernel_spmd

F32 = mybir.dt.float32
BF16 = mybir.dt.bfloat16
AF = mybir.ActivationFunctionType
ALU = mybir.AluOpType

D = 1024
S = 2048
L = 4
DFF = 3072
NCORES = 8
SPC = 2
G = 1024
N = 512
NT = G // N
KC = D // 128
HW = 516
EPS = 1e-6

P_G1, P_CW, P_CB, P_BA, P_BX, P_LAM, P_SW, P_GA, P_GB, P_G2, P_MW, P_MB, P_FG, NPV = (
    0, 8, 24, 28, 32, 36, 40, 52, 56, 60, 68, 140, 164, 172)


class Eng:
    def __init__(self, name, h, sem, is_dma=False):
        self.name, self.h, self.sem, self.is_dma = name, h, sem, is_dma
        self.count = 0
        self.waited = {}


class Buf:
    __slots__ = ("name", "w", "r")

    def __init__(self, name):
        self.name = name
        self.w = None
        self.r = {}


class Sched:
    def _deps(self, reads, writes):
        deps = {}

        def add(e, t):
            if e.is_dma:
                t = e.count
            if t > deps.get(e, 0):
                deps[e] = t
        for b in reads:
            if b.w is not None:
                add(*b.w)
        for b in writes:
            if b.w is not None:
                add(*b.w)
            for e, t in b.r.items():
                add(e, t)
        return deps

    def _wait(self, eng, deps):
        for e, t in deps.items():
            if e is eng and eng.name == "pe":
                continue
            if t > eng.waited.get(e, 0):
                eng.h.wait_ge(e.sem, t)
                eng.waited[e] = t

    def _mark(self, eng, tk, reads, writes):
        for b in reads:
            if tk > b.r.get(eng, 0):
                b.r[eng] = tk
        for b in writes:
            b.w = (eng, tk)
            b.r = {}

    def op(self, eng, reads, writes, fn):
        self._wait(eng, self._deps(reads, writes))
        ins = fn()
        eng.count += 1
        ins.then_inc(eng.sem, 1)
        self._mark(eng, eng.count, reads, writes)

    def mm(self, pe, reads, bank, mms, nc):
        self._wait(pe, self._deps(reads, [bank[0]]))
        n = len(mms)
        ins = None
        for i, (lhsT, rhs) in enumerate(mms):
            ins = nc.tensor.matmul(bank[1], lhsT, rhs, start=(i == 0), stop=(i == n - 1))
        pe.count += 1
        ins.then_inc(pe.sem, 1)
        self._mark(pe, pe.count, reads, [bank[0]])

    def dma(self, q, chan, reads, writes, fn):
        self._wait(q, self._deps(reads, writes))
        ins = fn()
        chan.count += 16
        ins.then_inc(chan.sem, 16)
        self._mark(chan, chan.count, reads, writes)


BLK_OFFS = (0, 512, 1536, 2048, 1024)


def build_nc(nlayers=L, ngroups=SPC * (S // G), pool_offload=True):
    nc = bass.Bass("TRN2", target_bir_lowering=False)
    es = ExitStack()

    def dram(name, shape, kind="ExternalInput"):
        return nc.dram_tensor(name, shape, F32, kind=kind).ap()

    xT = dram("xT", [SPC, D, S])
    pvec = dram("pvec", [128, L, NPV])
    w_in = dram("w_in", [L, 4, 128, 5120])
    w_a = dram("lru_w_a", [L, 8, 64, 64])
    w_x = dram("lru_w_x", [L, 8, 64, 64])
    w_out = dram("w_out", [L, 128, 8192])
    w_up = dram("mlp_w_up", [L, 6, 128, 4096])
    w_gate = dram("mlp_w_gate", [L, 6, 128, 4096])
    w_down = dram("mlp_w_down", [L, 6, 128, 4096])
    yT = dram("yT", [SPC, D, S], kind="ExternalOutput")

    def sb(name, shape, dt=F32):
        return es.enter_context(nc.sbuf_tensor(name, shape, dt))

    def sem(name):
        return es.enter_context(nc.semaphore(name))

    pe = Eng("pe", nc.tensor, sem("s_pe"))
    act = Eng("act", nc.scalar, sem("s_act"))
    dve = Eng("dve", nc.vector, sem("s_dve"))
    pool = Eng("pool", nc.gpsimd, sem("s_pool"))
    sp = Eng("sp", nc.sync, sem("s_sp"))
    aux = pool if pool_offload else dve

    def chan(name):
        return Eng(name, None, sem("c_" + name), is_dma=True)
    sc = Sched()

    h_sb = sb("h", [128, KC, G])
    v_sb = sb("v", [128, KC, G], BF16)
    win_sb = [sb(f"win{i}", [128, KC * 640], BF16) for i in range(2)]
    wout_sb = sb("wout", [128, KC * D], BF16)
    bd_sb = sb("bd", [128, 2, 4, 128], BF16)
    mup_sb = [sb(f"mup{i}", [128, KC * 512], BF16) for i in range(2)]
    mgt_sb = [sb(f"mgt{i}", [128, KC * 512], BF16) for i in range(2)]
    mdn_sb = [sb(f"mdn{i}", [128, 4 * D], BF16) for i in range(2)]
    pv_sb = sb("pv", [128, L, NPV])
    der_sb = sb("der", [128, L, 5, 4])
    tmp_sb = sb("tmpc", [128, L, 4])
    cst_sb = sb("cst", [128, 4])
    ones_sb = sb("ones", [128, 128], BF16)
    st_lx = sb("st_lx", [128, L, 4, 3])
    st_h = sb("st_h", [128, L, 4])
    st_cv = sb("st_cv", [128, L, 4, 2])
    st_up = sb("st_up", [128, L, 24, 2])
    NW = 22
    wk_sb = [sb(f"wk{i}", [128, HW]) for i in range(NW)]
    yb_sb = [sb(f"yy{i}", [128, N], BF16) for i in range(16)]
    xabf_sb = [sb(f"xabf{i}", [128, N], BF16) for i in range(2)]
    sq_sb = [sb(f"sq{i}", [128, N], BF16) for i in range(2)]
    ps_t = [es.enter_context(nc.psum_tensor(f"ps{i}", [128, N], F32)) for i in range(8)]

    hb = [[Buf(f"h{k}_{t}") for t in range(NT)] for k in range(KC)]
    vb = [[Buf(f"v{k}_{t}") for t in range(NT)] for k in range(KC)]
    b_win = [Buf("win0"), Buf("win1")]
    b_wout, b_bd = Buf("wout"), Buf("bd")
    b_m = [Buf("m0"), Buf("m1")]
    b_md = [Buf("md0"), Buf("md1")]
    b_pv, b_der, b_tmp, b_cst, b_ones = Buf("pv"), Buf("der"), Buf("tmp"), Buf("cst"), Buf("ones")
    b_slx = [[Buf(f"slx{l}_{j}") for j in range(4)] for l in range(L)]
    b_sh = [[Buf(f"sh{l}_{j}") for j in range(4)] for l in range(L)]
    b_scv = [[Buf(f"scv{l}_{j}") for j in range(4)] for l in range(L)]
    b_sup = [[Buf(f"sup{l}_{f}") for f in range(24)] for l in range(L)]
    wk = [(Buf(f"wk{i}"), wk_sb[i]) for i in range(NW)]
    wh = [Buf(f"wh{i}") for i in range(NW)]
    yyb = [(Buf(f"yy{i}"), yb_sb[i]) for i in range(16)]
    xabf = [(Buf(f"xabf{i}"), xabf_sb[i]) for i in range(2)]
    sqs = [(Buf(f"sq{i}"), sq_sb[i]) for i in range(2)]
    banks = [(Buf(f"ps{i}"), ps_t[i][:, :]) for i in range(8)]
    bank_i = [0]

    def next_bank():
        b = banks[bank_i[0] % 8]
        bank_i[0] += 1
        return b
    sq_i = [0]

    def next_sq():
        s_ = sqs[sq_i[0] % 2]
        sq_i[0] += 1
        return s_

    R_L, R_X, R_R, R_I, R_T, R_A, R_E, R_G, R_C, R_V = range(10)

    def wr(p, r):
        return wk[p * 10 + r]
    SD, RS = wk[20], wk[21]
    def whb(p, r):
        return wh[p * 10 + r]
    MLP_U = [wr(0, R_L), wr(0, R_V), wr(1, R_L), wr(1, R_V)]
    MLP_UH = [whb(0, R_L), whb(0, R_V), whb(1, R_L), whb(1, R_V)]
    MLP_A = [wr(0, R_X), wr(0, R_R), wr(1, R_X), wr(1, R_R)]
    STAGE = [wr(0, R_T), wr(0, R_A), wr(0, R_E), wr(1, R_T), wr(1, R_A), wr(1, R_E)]
    GGB = [[(Buf(f"gg{t}_{p}"), wk_sb[t * 10 + R_G].bitcast(BF16)[:, p * HW:p * HW + N]) for p in range(2)]
           for t in range(2)]

    c_x, c_pv, c_wout, c_bd = chan("x"), chan("pv"), chan("wout"), chan("bd")
    c_win = [chan("win0"), chan("win1")]
    c_m = [chan("m0"), chan("m1")]
    c_md = [chan("md0"), chan("md1")]
    c_out = chan("out")

    def pcol(l, c):
        return pv_sb[:, l, c:c + 1]

    def dcol(l, i, j):
        return der_sb[:, l, i, j:j + 1]

    def tsl(t):
        return slice(t * N, (t + 1) * N)

    def big_dma(ch, buf, dst, src, b):
        sc.dma(pool, ch, [], [buf], lambda: nc.gpsimd.dma_start(
            out=dst[:, :].rearrange("p (a b) -> p a b", b=b), in_=src.rearrange("p (a b) -> p a b", b=b)))

    def load_win(l, j):
        s_ = j % 2
        big_dma(c_win[s_], b_win[s_], win_sb[s_], w_in[l, j], 1280)

    def load_wout(l):
        big_dma(c_wout, b_wout, wout_sb, w_out[l], 2048)

    def load_bd(l):
        for gi, wsrc in enumerate((w_a, w_x)):
            src = wsrc[l].rearrange("(j two) d e -> two d j e", two=2)
            for two in range(2):
                sc.dma(pool, c_bd, [], [b_bd], lambda gi=gi, two=two, src=src: nc.gpsimd.dma_start(
                    out=bd_sb[two * 64:(two + 1) * 64, gi, :, two * 64:(two + 1) * 64], in_=src[two]))

    def load_mlp(l, c, s_, which):
        if which == 0:
            big_dma(c_m[s_], b_m[s_], mup_sb[s_], w_up[l, c], 2048)
            big_dma(c_m[s_], b_m[s_], mgt_sb[s_], w_gate[l, c], 2048)
        else:
            big_dma(c_md[s_], b_md[s_], mdn_sb[s_], w_down[l, c], 2048)

    sc.dma(sp, c_pv, [], [b_pv], lambda: nc.sync.dma_start(out=pv_sb[:, :, :], in_=pvec))
    sc.op(dve, [], [b_cst], lambda: nc.vector.memset(cst_sb[:, 0:1], EPS))
    sc.op(dve, [], [b_cst], lambda: nc.vector.memset(cst_sb[:, 1:2], 1.0))
    sc.op(dve, [], [b_ones], lambda: nc.vector.memset(ones_sb[:, :], 1.0))
    sc.op(pool, [], [b_bd], lambda: nc.gpsimd.memset(bd_sb[:, :, :, :], 0.0))
    lam = pv_sb[:, :, P_LAM:P_LAM + 4]
    sc.op(act, [b_pv], [b_tmp], lambda: nc.scalar.activation(out=tmp_sb[:, :, :], in_=lam, func=AF.Exp, scale=-1.0))
    sc.op(act, [b_tmp, b_cst], [b_tmp], lambda: nc.scalar.activation(
        out=tmp_sb[:, :, :], in_=tmp_sb[:, :, :], func=AF.Ln, bias=cst_sb[:, 1:2], scale=1.0))
    sc.op(dve, [b_pv], [b_der], lambda: nc.vector.tensor_scalar(
        out=der_sb[:, :, 0, :], in0=pv_sb[:, :, P_BA:P_BA + 4], scalar1=0.5, scalar2=None, op0=ALU.mult))
    sc.op(dve, [b_pv], [b_der], lambda: nc.vector.tensor_scalar(
        out=der_sb[:, :, 1, :], in0=pv_sb[:, :, P_BX:P_BX + 4], scalar1=0.5, scalar2=None, op0=ALU.mult))
    for idx, mul in ((2, 4.0), (3, -4.0), (4, -8.0)):
        sc.op(dve, [b_tmp], [b_der], lambda idx=idx, mul=mul: nc.vector.tensor_scalar(
            out=der_sb[:, :, idx, :], in0=tmp_sb[:, :, :], scalar1=mul, scalar2=None, op0=ALU.mult))

    lgs = [(g, l) for g in range(ngroups) for l in range(nlayers)]
    load_win(0, 0)
    load_win(0, 1)
    load_bd(0)
    load_wout(0)
    load_mlp(0, 0, 0, 0)
    load_mlp(0, 0, 0, 1)
    load_mlp(0, 1, 1, 0)
    load_mlp(0, 1, 1, 1)

    def aux_copy(reads, writes, out, in_):
        if False:
            sc.op(pool, reads, writes, lambda: nc.gpsimd.tensor_copy(out=out, in_=in_))
        else:
            sc.op(dve, reads, writes, lambda: nc.vector.tensor_copy(out=out, in_=in_))

    def aux_mul(reads, writes, out, in0, in1):
        if False:
            sc.op(pool, reads, writes, lambda: nc.gpsimd.tensor_tensor(out=out, in0=in0, in1=in1, op=ALU.mult))
        else:
            sc.op(dve, reads, writes, lambda: nc.vector.tensor_tensor(out=out, in0=in0, in1=in1, op=ALU.mult))

    def norm_stats(srcs, scale):
        bank = next_bank()
        n = len(srcs)
        for i, (b, ap) in enumerate(srcs):
            sqb, sqt = next_sq()
            sc.op(act, [b], [sqb], lambda ap=ap, sqt=sqt: nc.scalar.activation(out=sqt[:, :], in_=ap, func=AF.Square))
            sc._wait(pe, sc._deps([b_ones, sqb], [bank[0]]))
            ins = nc.tensor.matmul(bank[1], ones_sb[:, :], sqt[:, :], start=(i == 0), stop=(i == n - 1))
            pe.count += 1
            ins.then_inc(pe.sem, 1)
            sc._mark(pe, pe.count, [b_ones, sqb], [bank[0]])
        sc.op(act, [bank[0], b_cst], [SD[0]], lambda: nc.scalar.activation(
            out=SD[1][:, 0:N], in_=bank[1], func=AF.Ln, bias=cst_sb[:, 0:1], scale=scale))
        sc.op(act, [SD[0]], [RS[0]], lambda: nc.scalar.activation(
            out=RS[1][:, 0:N], in_=SD[1][:, 0:N], func=AF.Exp, scale=-0.5))
        return RS

    def norm_h(l, t, gcol):
        srcs = [(hb[k][t], h_sb[:, k, tsl(t)]) for k in range(KC)]
        rsb, rst = norm_stats(srcs, 1.0 / D)
        for k in range(KC):
            sc.op(dve, [hb[k][t], rsb, b_pv], [vb[k][t]], lambda k=k: nc.vector.scalar_tensor_tensor(
                out=v_sb[:, k, tsl(t)], in0=h_sb[:, k, tsl(t)], scalar=pcol(l, gcol + k), in1=rst[:, 0:N],
                op0=ALU.mult, op1=ALU.mult))

    def proj(l, t, j, blks=range(5)):
        bks = []
        s_ = j % 2
        for blk in blks:
            bank = next_bank()
            col = blk * 128
            sc.mm(pe, [b_win[s_]] + [vb[k][t] for k in range(KC)], bank,
                  [(win_sb[s_][:, k * 640 + col:k * 640 + col + 128], v_sb[:, k, tsl(t)]) for k in range(KC)], nc)
            bks.append(bank)
        return bks

    def early_conv(l, t, j, bA):
        Lb, Xb = wr(t, R_L), wr(t, R_X)
        xb_, xt_ = xabf[t]
        sc.op(act, [bA[0]], [Lb[0]], lambda: nc.scalar.activation(out=Lb[1][:, 3:3 + N], in_=bA[1], func=AF.Copy))
        Lh = whb(t, R_L)
        aux_copy([b_slx[l][j]], [Lh], Lb[1][:, 0:3], st_lx[:, l, j, :])
        sc.op(act, [Lb[0], b_pv], [Xb[0]], lambda: nc.scalar.activation(
            out=Xb[1][:, 0:N], in_=Lb[1][:, 3:3 + N], func=AF.Identity,
            bias=pcol(l, P_CB + j), scale=pcol(l, P_CW + j * 4 + 3)))
        for k in range(3):
            sc.op(dve, [Lb[0], Lh, Xb[0], b_pv], [Xb[0]], lambda k=k: nc.vector.scalar_tensor_tensor(
                out=Xb[1][:, 0:N], in0=Lb[1][:, k:k + N], scalar=pcol(l, P_CW + j * 4 + k), in1=Xb[1][:, 0:N],
                op0=ALU.mult, op1=ALU.add))
        aux_copy([Lb[0]], [b_slx[l][j]], st_lx[:, l, j, :], Lb[1][:, N:N + 3])
        sc.op(act, [Xb[0]], [xb_], lambda: nc.scalar.activation(out=xt_[:, :], in_=Xb[1][:, 0:N], func=AF.Copy))

    def early_rest(l, t, j, bks):
        bG, bC, bV, bB = bks
        C1, Vb = wr(t, R_C), wr(t, R_V)
        Gg = GGB[t][j % 2]
        Vh = whb(t, R_V)
        sc.op(act, [bG[0]], [Gg[0]], lambda: nc.scalar.activation(out=Gg[1], in_=bG[1], func=AF.Gelu_apprx_tanh))
        sc.op(act, [bC[0]], [C1[0]], lambda: nc.scalar.activation(out=C1[1][:, 0:N], in_=bC[1], func=AF.Copy))
        sc.op(dve, [C1[0], bV[0]], [Vb[0]], lambda: nc.vector.tensor_tensor(
            out=Vb[1][:, 2:2 + N], in0=C1[1][:, 0:N], in1=bV[1], op=ALU.mult))
        aux_copy([b_scv[l][j]], [Vh], Vb[1][:, 0:2], st_cv[:, l, j, :])
        sc.op(act, [Vb[0], b_pv], [C1[0]], lambda: nc.scalar.activation(
            out=C1[1][:, 0:N], in_=Vb[1][:, 2:2 + N], func=AF.Identity, scale=pcol(l, P_SW + j * 3 + 2)))
        for k in range(2):
            sc.op(dve, [Vb[0], Vh, C1[0], b_pv], [C1[0]], lambda k=k: nc.vector.scalar_tensor_tensor(
                out=C1[1][:, 0:N], in0=Vb[1][:, k:k + N], scalar=pcol(l, P_SW + j * 3 + k), in1=C1[1][:, 0:N],
                op0=ALU.mult, op1=ALU.add))
        aux_copy([Vb[0]], [b_scv[l][j]], st_cv[:, l, j, :], Vb[1][:, N:N + 2])
        ybb, ybt = yyb[t * 8 + 4 + j]
        sc.op(dve, [C1[0], bB[0]], [ybb], lambda: nc.vector.tensor_tensor(
            out=ybt[:, :], in0=C1[1][:, 0:N], in1=bB[1], op=ALU.mult))

    def ri_mm(l, t, j):
        xb_, xt_ = xabf[t]
        Rb = next_bank()
        sc.mm(pe, [b_bd, xb_], Rb, [(bd_sb[:, 0, j, :], xt_[:, :])], nc)
        Ib = next_bank()
        sc.mm(pe, [b_bd, xb_], Ib, [(bd_sb[:, 1, j, :], xt_[:, :])], nc)
        return Rb, Ib

    def late_a1(l, t, j, Rb, Ib):
        Rr, Ii = wr(t, R_R), wr(t, R_I)
        sc.op(act, [Rb[0], b_der], [Rr[0]], lambda: nc.scalar.activation(
            out=Rr[1][:, 0:N], in_=Rb[1], func=AF.Tanh, bias=dcol(l, 0, j), scale=0.5))
        sc.op(act, [Ib[0], b_der], [Ii[0]], lambda: nc.scalar.activation(
            out=Ii[1][:, 0:N], in_=Ib[1], func=AF.Tanh, bias=dcol(l, 1, j), scale=0.5))
        Xb = wr(t, R_X)
        sc.op(dve, [Ii[0], Xb[0]], [Ii[0]], lambda: nc.vector.scalar_tensor_tensor(
            out=Ii[1][:, 0:N], in0=Ii[1][:, 0:N], scalar=1.0, in1=Xb[1][:, 0:N], op0=ALU.add, op1=ALU.mult))

    def late_a(l, t, j):
        Rr, Ii, Tt, Aa, Ee = wr(t, R_R), wr(t, R_I), wr(t, R_T), wr(t, R_A), wr(t, R_E)
        sc.op(act, [Rr[0], b_der], [Tt[0]], lambda: nc.scalar.activation(
            out=Tt[1][:, 0:N], in_=Rr[1][:, 0:N], func=AF.Tanh, bias=dcol(l, 2, j), scale=dcol(l, 2, j)))
        sc.op(act, [Rr[0], b_der], [Aa[0]], lambda: nc.scalar.activation(
            out=Aa[1][:, 0:N], in_=Rr[1][:, 0:N], func=AF.Exp, bias=dcol(l, 3, j), scale=dcol(l, 3, j)))
        sc.op(act, [Rr[0], b_der], [Ee[0]], lambda: nc.scalar.activation(
            out=Ee[1][:, 0:N], in_=Rr[1][:, 0:N], func=AF.Exp, bias=dcol(l, 4, j), scale=dcol(l, 4, j)))
        sc.op(dve, [Ee[0], Tt[0]], [Tt[0]], lambda: nc.vector.scalar_tensor_tensor(
            out=Tt[1][:, 0:N], in0=Ee[1][:, 0:N], scalar=1.0, in1=Tt[1][:, 0:N], op0=ALU.add, op1=ALU.mult))

    def late_b0(l, t, j):
        Tt = wr(t, R_T)
        sc.op(act, [Tt[0]], [Tt[0]], lambda: nc.scalar.activation(
            out=Tt[1][:, 0:N], in_=Tt[1][:, 0:N], func=AF.Sqrt, scale=0.25))

    def late_b1(l, t, j):
        Ii, Tt, Aa, Ee = wr(t, R_I), wr(t, R_T), wr(t, R_A), wr(t, R_E)
        Gg = GGB[t][j % 2]
        sc.op(dve, [Ii[0], Tt[0]], [Ii[0]], lambda: nc.vector.tensor_tensor(
            out=Ii[1][:, 0:N], in0=Ii[1][:, 0:N], in1=Tt[1][:, 0:N], op=ALU.mult))
        sc.op(dve, [Aa[0], Ii[0], b_sh[l][j]], [Ee[0]], lambda: nc.vector.tensor_tensor_scan(
            out=Ee[1][:, 0:N], data0=Aa[1][:, 0:N], data1=Ii[1][:, 0:N], initial=st_h[:, l, j:j + 1],
            op0=ALU.mult, op1=ALU.add))
        aux_copy([Ee[0]], [b_sh[l][j]], st_h[:, l, j:j + 1], Ee[1][:, N - 1:N])
        yab, yat = yyb[t * 8 + j]
        aux_mul([Ee[0], Gg[0]], [yab], yat[:, :], Ee[1][:, 0:N], Gg[1])

    def gnorm(l, t, grps=(0, 1)):
        for grp in grps:
            tiles = [yyb[t * 8 + grp * 4 + j] for j in range(4)]
            rsb, rst = norm_stats([(b, tt_[:, :]) for (b, tt_) in tiles], 1.0 / 512)
            gcol = P_GA if grp == 0 else P_GB
            for j in range(4):
                yb_, yt_ = tiles[j]
                sc.op(dve, [yb_, rsb, b_pv], [yb_], lambda yt_=yt_, j=j, gcol=gcol: nc.vector.scalar_tensor_tensor(
                    out=yt_[:, :], in0=yt_[:, :], scalar=pcol(l, gcol + j), in1=rst[:, 0:N],
                    op0=ALU.mult, op1=ALU.mult))

    def outproj(l, t):
        ys = [yyb[t * 8 + k] for k in range(KC)]
        for oc in range(KC):
            bank = next_bank()
            sc.mm(pe, [b_wout] + [y[0] for y in ys], bank,
                  [(wout_sb[:, k * D + oc * 128:k * D + (oc + 1) * 128], ys[k][1][:, :]) for k in range(KC)], nc)
            sc.op(dve, [bank[0], hb[oc][t]], [hb[oc][t]], lambda oc=oc, bank=bank: nc.vector.tensor_tensor(
                out=h_sb[:, oc, tsl(t)], in0=h_sb[:, oc, tsl(t)], in1=bank[1], op=ALU.add))

    def next_win_prefetch(lgi, j):
        q = j + 2
        if q < 4:
            load_win(lgs[lgi][1], q)
        elif lgi + 1 < len(lgs):
            load_win(lgs[lgi + 1][1], q - 4)

    def phase_a(l, lgi):
        nxt = lgs[lgi + 1] if lgi + 1 < len(lgs) else None
        for j in range(4):
            bks0 = proj(l, 0, j)
            early_conv(l, 0, j, bks0[0])
            early_rest(l, 0, j, bks0[1:])
            bA1 = proj(l, 1, j, [0])
            early_conv(l, 1, j, bA1[0])
            if j > 0:
                late_a(l, 0, j - 1)
                late_a(l, 1, j - 1)
                late_b0(l, 0, j - 1)
                late_b0(l, 1, j - 1)
                late_b1(l, 0, j - 1)
            ri0 = ri_mm(l, 0, j)
            late_a1(l, 0, j, *ri0)
            bks1 = proj(l, 1, j, [1, 2, 3, 4])
            next_win_prefetch(lgi, j)
            early_rest(l, 1, j, bks1)
            if j == 3:
                gnorm(l, 0, (1,))
            if j > 0:
                late_b1(l, 1, j - 1)
            ri1 = ri_mm(l, 1, j)
            late_a1(l, 1, j, *ri1)
        gnorm(l, 1, (1,))
        late_a(l, 0, 3)
        late_a(l, 1, 3)
        late_b0(l, 0, 3)
        late_b0(l, 1, 3)
        late_b1(l, 0, 3)
        late_b1(l, 1, 3)
        if nxt is not None:
            load_bd(nxt[1])
        gnorm(l, 0, (0,))
        outproj(l, 0)
        gnorm(l, 1, (0,))
        norm_h(l, 0, P_G2)
        outproj(l, 1)
        if nxt is not None:
            load_wout(nxt[1])

    mset = [0]

    def mlp_up(l, c, t, s_, aset):
        for fb in range(4):
            f = c * 4 + fb
            ws = mset[0] % 4
            mset[0] += 1
            Ub, Ab, Uh = MLP_U[ws], MLP_A[ws], MLP_UH[ws]
            bU = next_bank()
            sc.mm(pe, [b_m[s_]] + [vb[k][t] for k in range(KC)], bU,
                  [(mup_sb[s_][:, k * 512 + fb * 128:k * 512 + (fb + 1) * 128], v_sb[:, k, tsl(t)]) for k in range(KC)], nc)
            bGt = next_bank()
            sc.mm(pe, [b_m[s_]] + [vb[k][t] for k in range(KC)], bGt,
                  [(mgt_sb[s_][:, k * 512 + fb * 128:k * 512 + (fb + 1) * 128], v_sb[:, k, tsl(t)]) for k in range(KC)], nc)
            sc.op(act, [bU[0]], [Ub[0]], lambda: nc.scalar.activation(out=Ub[1][:, 2:2 + N], in_=bU[1], func=AF.Copy))
            sc.op(dve, [b_sup[l][f]], [Uh], lambda: nc.vector.tensor_copy(out=Ub[1][:, 0:2], in_=st_up[:, l, f, :]))
            sc.op(act, [Ub[0], b_pv], [Ab[0]], lambda: nc.scalar.activation(
                out=Ab[1][:, 0:N], in_=Ub[1][:, 2:2 + N], func=AF.Identity,
                bias=pcol(l, P_MB + f), scale=pcol(l, P_MW + f * 3 + 2)))
            for k in range(2):
                sc.op(dve, [Ub[0], Uh, Ab[0], b_pv], [Ab[0]], lambda k=k: nc.vector.scalar_tensor_tensor(
                    out=Ab[1][:, 0:N], in0=Ub[1][:, k:k + N], scalar=pcol(l, P_MW + f * 3 + k), in1=Ab[1][:, 0:N],
                    op0=ALU.mult, op1=ALU.add))
            sc.op(dve, [Ub[0]], [b_sup[l][f]], lambda: nc.vector.tensor_copy(out=st_up[:, l, f, :], in_=Ub[1][:, N:N + 2]))
            sc.op(act, [Ab[0]], [Ab[0]], lambda: nc.scalar.activation(
                out=Ab[1][:, 0:N], in_=Ab[1][:, 0:N], func=AF.Gelu_apprx_tanh))
            ab_, at_ = yyb[aset * 4 + fb]
            sc.op(dve, [Ab[0], bGt[0]], [ab_], lambda: nc.vector.tensor_tensor(
                out=at_[:, :], in0=Ab[1][:, 0:N], in1=bGt[1], op=ALU.mult))

    def mlp_down(l, c, t, s_, aset):
        acts = [yyb[aset * 4 + fb] for fb in range(4)]
        for oc in range(KC):
            bank = next_bank()
            sc.mm(pe, [b_md[s_]] + [a[0] for a in acts], bank,
                  [(mdn_sb[s_][:, fb * D + oc * 128:fb * D + (oc + 1) * 128], acts[fb][1][:, :]) for fb in range(4)], nc)
            sc.op(dve, [bank[0], hb[oc][t]], [hb[oc][t]], lambda oc=oc, bank=bank: nc.vector.tensor_tensor(
                out=h_sb[:, oc, tsl(t)], in0=h_sb[:, oc, tsl(t)], in1=bank[1], op=ALU.add))

    def final_tile(seq, half, t):
        srcs = [(hb[k][t], h_sb[:, k, tsl(t)]) for k in range(KC)]
        rsb, rst = norm_stats(srcs, 1.0 / D)
        for k in range(KC):
            ob, ot = STAGE[k % 6]
            sc.op(dve, [hb[k][t], rsb, b_pv], [ob], lambda k=k, ot=ot: nc.vector.scalar_tensor_tensor(
                out=ot[:, 0:N], in0=h_sb[:, k, tsl(t)], scalar=pcol(0, P_FG + k), in1=rst[:, 0:N],
                op0=ALU.mult, op1=ALU.mult))
            sc.dma(sp, c_out, [ob], [], lambda k=k, ot=ot: nc.sync.dma_start(
                out=yT[seq, k * 128:(k + 1) * 128, half * G + t * N: half * G + (t + 1) * N], in_=ot[:, 0:N]))

    for lgi, (g, l) in enumerate(lgs):
        seq, half = divmod(g, S // G)
        last_layer = (l == nlayers - 1)
        if l == 0:
            for k in range(KC):
                sc.dma(sp, c_x, [], [hb[k][t] for t in range(NT)], lambda k=k: nc.sync.dma_start(
                    out=h_sb[:, k, :], in_=xT[seq, k * 128:(k + 1) * 128, half * G:(half + 1) * G]))
            if half == 0:
                for ll in range(nlayers):
                    sc.op(dve, [], b_slx[ll], lambda ll=ll: nc.vector.memset(st_lx[:, ll, :, :], 0.0))
                    sc.op(dve, [], b_sh[ll], lambda ll=ll: nc.vector.memset(st_h[:, ll, :], 0.0))
                    sc.op(dve, [], b_scv[ll], lambda ll=ll: nc.vector.memset(st_cv[:, ll, :, :], 0.0))
                    sc.op(dve, [], b_sup[ll], lambda ll=ll: nc.vector.memset(st_up[:, ll, :, :], 0.0))
            norm_h(l, 0, P_G1)
            norm_h(l, 1, P_G1)
        phase_a(l, lgi)
        units = [(c, t) for c in range(6) for t in range(NT)]
        prev = None
        for ui, (c, t) in enumerate(units):
            s_ = c % 2
            aset = ui % 2
            mlp_up(l, c, t, s_, aset)
            if t == NT - 1:
                _prefetch_mlp(sc, lgs, lgi, c, s_, load_mlp, 0)
            if ui == 0:
                norm_h(l, 1, P_G2)
            if prev is not None:
                pc, pt, ps_, pa = prev
                mlp_down(l, pc, pt, ps_, pa)
                if pt == NT - 1:
                    _prefetch_mlp(sc, lgs, lgi, pc, ps_, load_mlp, 1)
            prev = (c, t, s_, aset)
        pc, pt, ps_, pa = prev
        if last_layer:
            final_tile(seq, half, 0)
        else:
            norm_h(l + 1, 0, P_G1)
        mlp_down(l, pc, pt, ps_, pa)
        _prefetch_mlp(sc, lgs, lgi, pc, ps_, load_mlp, 1)
        if last_layer:
            final_tile(seq, half, 1)
        else:
            norm_h(l + 1, 1, P_G1)
    nc.sync.wait_ge(c_out.sem, c_out.count)
    es.close()
    return nc


def _prefetch_mlp(sc, lgs, lgi, c, s_, load_mlp, which):
    nc_ = c + 2
    if nc_ < 6:
        load_mlp(lgs[lgi][1], nc_, s_, which)
    elif lgi + 1 < len(lgs):
        load_mlp(lgs[lgi + 1][1], nc_ - 6, s_, which)


def pack_params(inp):
    pv = np.zeros((128, L, NPV), np.float32)

    def ch(v, n):
        return np.ascontiguousarray(v.reshape(n, 128).T)
    for l in range(L):
        pv[:, l, P_G1:P_G1 + 8] = ch(inp["norm1_g"][l], 8)
        cw = inp["lru_conv_w"][l]
        for j in range(4):
            for k in range(4):
                pv[:, l, P_CW + j * 4 + k] = cw[k, j * 128:(j + 1) * 128]
        pv[:, l, P_CB:P_CB + 4] = ch(inp["lru_conv_b"][l], 4)
        pv[:, l, P_BA:P_BA + 4] = ch(inp["lru_b_a"][l].reshape(512), 4)
        pv[:, l, P_BX:P_BX + 4] = ch(inp["lru_b_x"][l].reshape(512), 4)
        pv[:, l, P_LAM:P_LAM + 4] = ch(inp["lru_lambda"][l], 4)
        sw = inp["sc_conv_w"][l]
        for j in range(4):
            for k in range(3):
                pv[:, l, P_SW + j * 3 + k] = sw[k, j * 128:(j + 1) * 128]
        pv[:, l, P_GA:P_GA + 4] = ch(inp["lru_out_g"][l], 4)
        pv[:, l, P_GB:P_GB + 4] = ch(inp["sc_out_g"][l], 4)
        pv[:, l, P_G2:P_G2 + 8] = ch(inp["norm2_g"][l], 8)
        mw = inp["mlp_conv_w"][l]
        for f in range(24):
            for k in range(3):
                pv[:, l, P_MW + f * 3 + k] = mw[k, f * 128:(f + 1) * 128]
        pv[:, l, P_MB:P_MB + 24] = ch(inp["mlp_conv_b"][l], 24)
        pv[:, l, P_FG:P_FG + 8] = ch(inp["final_g"], 8)
    return pv


def permute_w_in(w):
    out = np.empty_like(w)
    for j in range(4):
        for bi, off in enumerate(BLK_OFFS):
            out[:, :, j * 640 + bi * 128: j * 640 + (bi + 1) * 128] = w[:, :, off + j * 128: off + (j + 1) * 128]
    return out


def relayout_weights(inp):
    f = lambda a: np.asarray(a, dtype=np.float32)
    w_in_p = permute_w_in(f(inp["w_in"]))
    return {
        "w_in": np.ascontiguousarray(w_in_p.reshape(L, 8, 128, 4, 640).transpose(0, 3, 2, 1, 4)).reshape(L, 4, 128, 5120),
        "w_out": np.ascontiguousarray(f(inp["w_out"]).reshape(L, 8, 128, 1024).transpose(0, 2, 1, 3)).reshape(L, 128, 8192),
        "mlp_w_up": np.ascontiguousarray(f(inp["mlp_w_up"]).reshape(L, 8, 128, 6, 512).transpose(0, 3, 2, 1, 4)).reshape(L, 6, 128, 4096),
        "mlp_w_gate": np.ascontiguousarray(f(inp["mlp_w_gate"]).reshape(L, 8, 128, 6, 512).transpose(0, 3, 2, 1, 4)).reshape(L, 6, 128, 4096),
        "mlp_w_down": np.ascontiguousarray(f(inp["mlp_w_down"]).reshape(L, 6, 4, 128, 1024).transpose(0, 1, 3, 2, 4)).reshape(L, 6, 128, 4096),
        "lru_w_a": np.ascontiguousarray(f(inp["lru_w_a"])),
        "lru_w_x": np.ascontiguousarray(f(inp["lru_w_x"])),
    }


_NC_CACHE = {}


def kernel(**inp):
    inp = {k: np.asarray(v) for k, v in inp.items()}
    x = inp["x"].astype(np.float32, copy=False)
    if "full" not in _NC_CACHE:
        _NC_CACHE["full"] = build_nc()
    nc = _NC_CACHE["full"]
    pv = pack_params(inp)
    shared = relayout_weights(inp)
    shared["pvec"] = pv
    in_maps = []
    for c in range(NCORES):
        xs = x[c * SPC:(c + 1) * SPC]
        m = dict(shared)
        m["xT"] = np.ascontiguousarray(xs.transpose(0, 2, 1))
        in_maps.append(m)
    res = run_bass_kernel_spmd(nc, in_maps, core_ids=list(range(NCORES)))
    out = np.empty((NCORES * SPC, S, D), np.float32)
    for c in range(NCORES):
        out[c * SPC:(c + 1) * SPC] = res.results[c]["yT"].transpose(0, 2, 1)
    return out
```

```python
import numpy as np
from contextlib import ExitStack
import concourse.bass as bass
import concourse.mybir as mybir
from concourse.bass_utils import run_bass_kernel_spmd

F32 = mybir.dt.float32
BF16 = mybir.dt.bfloat16
AF = mybir.ActivationFunctionType
ALU = mybir.AluOpType

D = 1024
S = 2048
L = 4
DFF = 3072
NCORES = 8
SPC = 2
G = 1024
N = 512
NT = G // N
KC = D // 128
HW = 516
EPS = 1e-6

P_G1, P_CW, P_CB, P_BA, P_BX, P_LAM, P_SW, P_GA, P_GB, P_G2, P_MW, P_MB, P_FG, NPV = (
    0, 8, 24, 28, 32, 36, 40, 52, 56, 60, 68, 140, 164, 172)


class Eng:
    def __init__(self, name, h, sem, is_dma=False):
        self.name, self.h, self.sem, self.is_dma = name, h, sem, is_dma
        self.count = 0
        self.waited = {}


class Buf:
    __slots__ = ("name", "w", "r")

    def __init__(self, name):
        self.name = name
        self.w = None
        self.r = {}


class Sched:
    def _deps(self, reads, writes):
        deps = {}

        def add(e, t):
            if e.is_dma:
                t = e.count
            if t > deps.get(e, 0):
                deps[e] = t
        for b in reads:
            if b.w is not None:
                add(*b.w)
        for b in writes:
            if b.w is not None:
                add(*b.w)
            for e, t in b.r.items():
                add(e, t)
        return deps

    def _wait(self, eng, deps):
        for e, t in deps.items():
            if e is eng and eng.name == "pe":
                continue
            if t > eng.waited.get(e, 0):
                eng.h.wait_ge(e.sem, t)
                eng.waited[e] = t

    def _mark(self, eng, tk, reads, writes):
        for b in reads:
            if tk > b.r.get(eng, 0):
                b.r[eng] = tk
        for b in writes:
            b.w = (eng, tk)
            b.r = {}

    def op(self, eng, reads, writes, fn):
        self._wait(eng, self._deps(reads, writes))
        ins = fn()
        eng.count += 1
        ins.then_inc(eng.sem, 1)
        self._mark(eng, eng.count, reads, writes)

    def mm(self, pe, reads, bank, mms, nc):
        self._wait(pe, self._deps(reads, [bank[0]]))
        n = len(mms)
        ins = None
        for i, (lhsT, rhs) in enumerate(mms):
            ins = nc.tensor.matmul(bank[1], lhsT, rhs, start=(i == 0), stop=(i == n - 1))
        pe.count += 1
        ins.then_inc(pe.sem, 1)
        self._mark(pe, pe.count, reads, [bank[0]])

    def dma(self, q, chan, reads, writes, fn):
        self._wait(q, self._deps(reads, writes))
        ins = fn()
        chan.count += 16
        ins.then_inc(chan.sem, 16)
        self._mark(chan, chan.count, reads, writes)


BLK_OFFS = (0, 512, 1536, 2048, 1024)


def build_nc(nlayers=L, ngroups=SPC * (S // G), pool_offload=True):
    nc = bass.Bass("TRN2", target_bir_lowering=False)
    es = ExitStack()

    def dram(name, shape, kind="ExternalInput"):
        return nc.dram_tensor(name, shape, F32, kind=kind).ap()

    xT = dram("xT", [SPC, D, S])
    pvec = dram("pvec", [128, L, NPV])
    w_in = dram("w_in", [L, 4, 128, 5120])
    w_a = dram("lru_w_a", [L, 8, 64, 64])
    w_x = dram("lru_w_x", [L, 8, 64, 64])
    w_out = dram("w_out", [L, 128, 8192])
    w_up = dram("mlp_w_up", [L, 6, 128, 4096])
    w_gate = dram("mlp_w_gate", [L, 6, 128, 4096])
    w_down = dram("mlp_w_down", [L, 6, 128, 4096])
    yT = dram("yT", [SPC, D, S], kind="ExternalOutput")

    def sb(name, shape, dt=F32):
        return es.enter_context(nc.sbuf_tensor(name, shape, dt))

    def sem(name):
        return es.enter_context(nc.semaphore(name))

    pe = Eng("pe", nc.tensor, sem("s_pe"))
    act = Eng("act", nc.scalar, sem("s_act"))
    dve = Eng("dve", nc.vector, sem("s_dve"))
    pool = Eng("pool", nc.gpsimd, sem("s_pool"))
    sp = Eng("sp", nc.sync, sem("s_sp"))
    aux = pool if pool_offload else dve

    def chan(name):
        return Eng(name, None, sem("c_" + name), is_dma=True)
    sc = Sched()

    h_sb = sb("h", [128, KC, G])
    v_sb = sb("v", [128, KC, G], BF16)
    win_sb = [sb(f"win{i}", [128, KC * 640], BF16) for i in range(2)]
    wout_sb = sb("wout", [128, KC * D], BF16)
    bd_sb = sb("bd", [128, 2, 4, 128], BF16)
    mup_sb = [sb(f"mup{i}", [128, KC * 512], BF16) for i in range(2)]
    mgt_sb = [sb(f"mgt{i}", [128, KC * 512], BF16) for i in range(2)]
    mdn_sb = [sb(f"mdn{i}", [128, 4 * D], BF16) for i in range(2)]
    pv_sb = sb("pv", [128, L, NPV])
    der_sb = sb("der", [128, L, 5, 4])
    tmp_sb = sb("tmpc", [128, L, 4])
    cst_sb = sb("cst", [128, 4])
    ones_sb = sb("ones", [128, 128], BF16)
    st_lx = sb("st_lx", [128, L, 4, 3])
    st_h = sb("st_h", [128, L, 4])
    st_cv = sb("st_cv", [128, L, 4, 2])
    st_up = sb("st_up", [128, L, 24, 2])
    NW = 22
    wk_sb = [sb(f"wk{i}", [128, HW]) for i in range(NW)]
    yb_sb = [sb(f"yy{i}", [128, N], BF16) for i in range(16)]
    xabf_sb = [sb(f"xabf{i}", [128, N], BF16) for i in range(2)]
    sq_sb = [sb(f"sq{i}", [128, N], BF16) for i in range(2)]
    ps_t = [es.enter_context(nc.psum_tensor(f"ps{i}", [128, N], F32)) for i in range(8)]

    hb = [[Buf(f"h{k}_{t}") for t in range(NT)] for k in range(KC)]
    vb = [[Buf(f"v{k}_{t}") for t in range(NT)] for k in range(KC)]
    b_win = [Buf("win0"), Buf("win1")]
    b_wout, b_bd = Buf("wout"), Buf("bd")
    b_m = [Buf("m0"), Buf("m1")]
    b_md = [Buf("md0"), Buf("md1")]
    b_pv, b_der, b_tmp, b_cst, b_ones = Buf("pv"), Buf("der"), Buf("tmp"), Buf("cst"), Buf("ones")
    b_slx = [[Buf(f"slx{l}_{j}") for j in range(4)] for l in range(L)]
    b_sh = [[Buf(f"sh{l}_{j}") for j in range(4)] for l in range(L)]
    b_scv = [[Buf(f"scv{l}_{j}") for j in range(4)] for l in range(L)]
    b_sup = [[Buf(f"sup{l}_{f}") for f in range(24)] for l in range(L)]
    wk = [(Buf(f"wk{i}"), wk_sb[i]) for i in range(NW)]
    wh = [Buf(f"wh{i}") for i in range(NW)]
    yyb = [(Buf(f"yy{i}"), yb_sb[i]) for i in range(16)]
    xabf = [(Buf(f"xabf{i}"), xabf_sb[i]) for i in range(2)]
    sqs = [(Buf(f"sq{i}"), sq_sb[i]) for i in range(2)]
    banks = [(Buf(f"ps{i}"), ps_t[i][:, :]) for i in range(8)]
    bank_i = [0]

    def next_bank():
        b = banks[bank_i[0] % 8]
        bank_i[0] += 1
        return b
    sq_i = [0]

    def next_sq():
        s_ = sqs[sq_i[0] % 2]
        sq_i[0] += 1
        return s_

    R_L, R_X, R_R, R_I, R_T, R_A, R_E, R_G, R_C, R_V = range(10)

    def wr(p, r):
        return wk[p * 10 + r]
    SD, RS = wk[20], wk[21]
    def whb(p, r):
        return wh[p * 10 + r]
    MLP_U = [wr(0, R_L), wr(0, R_V), wr(1, R_L), wr(1, R_V)]
    MLP_UH = [whb(0, R_L), whb(0, R_V), whb(1, R_L), whb(1, R_V)]
    MLP_A = [wr(0, R_X), wr(0, R_R), wr(1, R_X), wr(1, R_R)]
    STAGE = [wr(0, R_T), wr(0, R_A), wr(0, R_E), wr(1, R_T), wr(1, R_A), wr(1, R_E)]
    GGB = [[(Buf(f"gg{t}_{p}"), wk_sb[t * 10 + R_G].bitcast(BF16)[:, p * HW:p * HW + N]) for p in range(2)]
           for t in range(2)]

    c_x, c_pv, c_wout, c_bd = chan("x"), chan("pv"), chan("wout"), chan("bd")
    c_win = [chan("win0"), chan("win1")]
    c_m = [chan("m0"), chan("m1")]
    c_md = [chan("md0"), chan("md1")]
    c_out = chan("out")

    def pcol(l, c):
        return pv_sb[:, l, c:c + 1]

    def dcol(l, i, j):
        return der_sb[:, l, i, j:j + 1]

    def tsl(t):
        return slice(t * N, (t + 1) * N)

    def big_dma(ch, buf, dst, src, b):
        sc.dma(pool, ch, [], [buf], lambda: nc.gpsimd.dma_start(
            out=dst[:, :].rearrange("p (a b) -> p a b", b=b), in_=src.rearrange("p (a b) -> p a b", b=b)))

    def load_win(l, j):
        s_ = j % 2
        big_dma(c_win[s_], b_win[s_], win_sb[s_], w_in[l, j], 1280)

    def load_wout(l):
        big_dma(c_wout, b_wout, wout_sb, w_out[l], 2048)

    def load_bd(l):
        for gi, wsrc in enumerate((w_a, w_x)):
            src = wsrc[l].rearrange("(j two) d e -> two d j e", two=2)
            for two in range(2):
                sc.dma(pool, c_bd, [], [b_bd], lambda gi=gi, two=two, src=src: nc.gpsimd.dma_start(
                    out=bd_sb[two * 64:(two + 1) * 64, gi, :, two * 64:(two + 1) * 64], in_=src[two]))

    def load_mlp(l, c, s_, which):
        if which == 0:
            big_dma(c_m[s_], b_m[s_], mup_sb[s_], w_up[l, c], 2048)
            big_dma(c_m[s_], b_m[s_], mgt_sb[s_], w_gate[l, c], 2048)
        else:
            big_dma(c_md[s_], b_md[s_], mdn_sb[s_], w_down[l, c], 2048)

    sc.dma(sp, c_pv, [], [b_pv], lambda: nc.sync.dma_start(out=pv_sb[:, :, :], in_=pvec))
    sc.op(dve, [], [b_cst], lambda: nc.vector.memset(cst_sb[:, 0:1], EPS))
    sc.op(dve, [], [b_cst], lambda: nc.vector.memset(cst_sb[:, 1:2], 1.0))
    sc.op(dve, [], [b_ones], lambda: nc.vector.memset(ones_sb[:, :], 1.0))
    sc.op(pool, [], [b_bd], lambda: nc.gpsimd.memset(bd_sb[:, :, :, :], 0.0))
    lam = pv_sb[:, :, P_LAM:P_LAM + 4]
    sc.op(act, [b_pv], [b_tmp], lambda: nc.scalar.activation(out=tmp_sb[:, :, :], in_=lam, func=AF.Exp, scale=-1.0))
    sc.op(act, [b_tmp, b_cst], [b_tmp], lambda: nc.scalar.activation(
        out=tmp_sb[:, :, :], in_=tmp_sb[:, :, :], func=AF.Ln, bias=cst_sb[:, 1:2], scale=1.0))
    sc.op(dve, [b_pv], [b_der], lambda: nc.vector.tensor_scalar(
        out=der_sb[:, :, 0, :], in0=pv_sb[:, :, P_BA:P_BA + 4], scalar1=0.5, scalar2=None, op0=ALU.mult))
    sc.op(dve, [b_pv], [b_der], lambda: nc.vector.tensor_scalar(
        out=der_sb[:, :, 1, :], in0=pv_sb[:, :, P_BX:P_BX + 4], scalar1=0.5, scalar2=None, op0=ALU.mult))
    for idx, mul in ((2, 4.0), (3, -4.0), (4, -8.0)):
        sc.op(dve, [b_tmp], [b_der], lambda idx=idx, mul=mul: nc.vector.tensor_scalar(
            out=der_sb[:, :, idx, :], in0=tmp_sb[:, :, :], scalar1=mul, scalar2=None, op0=ALU.mult))

    lgs = [(g, l) for g in range(ngroups) for l in range(nlayers)]
    load_win(0, 0)
    load_win(0, 1)
    load_bd(0)
    load_wout(0)
    load_mlp(0, 0, 0, 0)
    load_mlp(0, 0, 0, 1)
    load_mlp(0, 1, 1, 0)
    load_mlp(0, 1, 1, 1)

    def aux_copy(reads, writes, out, in_):
        if False:
            sc.op(pool, reads, writes, lambda: nc.gpsimd.tensor_copy(out=out, in_=in_))
        else:
            sc.op(dve, reads, writes, lambda: nc.vector.tensor_copy(out=out, in_=in_))

    def aux_mul(reads, writes, out, in0, in1):
        if False:
            sc.op(pool, reads, writes, lambda: nc.gpsimd.tensor_tensor(out=out, in0=in0, in1=in1, op=ALU.mult))
        else:
            sc.op(dve, reads, writes, lambda: nc.vector.tensor_tensor(out=out, in0=in0, in1=in1, op=ALU.mult))

    def norm_stats(srcs, scale):
        bank = next_bank()
        n = len(srcs)
        for i, (b, ap) in enumerate(srcs):
            sqb, sqt = next_sq()
            sc.op(act, [b], [sqb], lambda ap=ap, sqt=sqt: nc.scalar.activation(out=sqt[:, :], in_=ap, func=AF.Square))
            sc._wait(pe, sc._deps([b_ones, sqb], [bank[0]]))
            ins = nc.tensor.matmul(bank[1], ones_sb[:, :], sqt[:, :], start=(i == 0), stop=(i == n - 1))
            pe.count += 1
            ins.then_inc(pe.sem, 1)
            sc._mark(pe, pe.count, [b_ones, sqb], [bank[0]])
        sc.op(act, [bank[0], b_cst], [SD[0]], lambda: nc.scalar.activation(
            out=SD[1][:, 0:N], in_=bank[1], func=AF.Ln, bias=cst_sb[:, 0:1], scale=scale))
        sc.op(act, [SD[0]], [RS[0]], lambda: nc.scalar.activation(
            out=RS[1][:, 0:N], in_=SD[1][:, 0:N], func=AF.Exp, scale=-0.5))
        return RS

    def norm_h(l, t, gcol):
        srcs = [(hb[k][t], h_sb[:, k, tsl(t)]) for k in range(KC)]
        rsb, rst = norm_stats(srcs, 1.0 / D)
        for k in range(KC):
            sc.op(dve, [hb[k][t], rsb, b_pv], [vb[k][t]], lambda k=k: nc.vector.scalar_tensor_tensor(
                out=v_sb[:, k, tsl(t)], in0=h_sb[:, k, tsl(t)], scalar=pcol(l, gcol + k), in1=rst[:, 0:N],
                op0=ALU.mult, op1=ALU.mult))

    def proj(l, t, j, blks=range(5)):
        bks = []
        s_ = j % 2
        for blk in blks:
            bank = next_bank()
            col = blk * 128
            sc.mm(pe, [b_win[s_]] + [vb[k][t] for k in range(KC)], bank,
                  [(win_sb[s_][:, k * 640 + col:k * 640 + col + 128], v_sb[:, k, tsl(t)]) for k in range(KC)], nc)
            bks.append(bank)
        return bks

    def early_conv(l, t, j, bA):
        Lb, Xb = wr(t, R_L), wr(t, R_X)
        xb_, xt_ = xabf[t]
        sc.op(act, [bA[0]], [Lb[0]], lambda: nc.scalar.activation(out=Lb[1][:, 3:3 + N], in_=bA[1], func=AF.Copy))
        Lh = whb(t, R_L)
        aux_copy([b_slx[l][j]], [Lh], Lb[1][:, 0:3], st_lx[:, l, j, :])
        sc.op(act, [Lb[0], b_pv], [Xb[0]], lambda: nc.scalar.activation(
            out=Xb[1][:, 0:N], in_=Lb[1][:, 3:3 + N], func=AF.Identity,
            bias=pcol(l, P_CB + j), scale=pcol(l, P_CW + j * 4 + 3)))
        for k in range(3):
            sc.op(dve, [Lb[0], Lh, Xb[0], b_pv], [Xb[0]], lambda k=k: nc.vector.scalar_tensor_tensor(
                out=Xb[1][:, 0:N], in0=Lb[1][:, k:k + N], scalar=pcol(l, P_CW + j * 4 + k), in1=Xb[1][:, 0:N],
                op0=ALU.mult, op1=ALU.add))
        aux_copy([Lb[0]], [b_slx[l][j]], st_lx[:, l, j, :], Lb[1][:, N:N + 3])
        sc.op(act, [Xb[0]], [xb_], lambda: nc.scalar.activation(out=xt_[:, :], in_=Xb[1][:, 0:N], func=AF.Copy))

    def egate(l, t, j, bG):
        Gg = GGB[t][j % 2]
        sc.op(act, [bG[0]], [Gg[0]], lambda: nc.scalar.activation(out=Gg[1], in_=bG[1], func=AF.Gelu_apprx_tanh))

    def early_b(l, t, j, bks):
        bC, bV, bB = bks
        C1, Vb = wr(t, R_C), wr(t, R_V)
        Vh = whb(t, R_V)
        sc.op(act, [bC[0]], [C1[0]], lambda: nc.scalar.activation(out=C1[1][:, 0:N], in_=bC[1], func=AF.Copy))
        sc.op(dve, [C1[0], bV[0]], [Vb[0]], lambda: nc.vector.tensor_tensor(
            out=Vb[1][:, 2:2 + N], in0=C1[1][:, 0:N], in1=bV[1], op=ALU.mult))
        aux_copy([b_scv[l][j]], [Vh], Vb[1][:, 0:2], st_cv[:, l, j, :])
        sc.op(act, [Vb[0], b_pv], [C1[0]], lambda: nc.scalar.activation(
            out=C1[1][:, 0:N], in_=Vb[1][:, 2:2 + N], func=AF.Identity, scale=pcol(l, P_SW + j * 3 + 2)))
        for k in range(2):
            sc.op(dve, [Vb[0], Vh, C1[0], b_pv], [C1[0]], lambda k=k: nc.vector.scalar_tensor_tensor(
                out=C1[1][:, 0:N], in0=Vb[1][:, k:k + N], scalar=pcol(l, P_SW + j * 3 + k), in1=C1[1][:, 0:N],
                op0=ALU.mult, op1=ALU.add))
        aux_copy([Vb[0]], [b_scv[l][j]], st_cv[:, l, j, :], Vb[1][:, N:N + 2])
        ybb, ybt = yyb[t * 8 + 4 + j]
        sc.op(dve, [C1[0], bB[0]], [ybb], lambda: nc.vector.tensor_tensor(
            out=ybt[:, :], in0=C1[1][:, 0:N], in1=bB[1], op=ALU.mult))

    def ri_mm(l, t, j):
        xb_, xt_ = xabf[t]
        Rb = next_bank()
        sc.mm(pe, [b_bd, xb_], Rb, [(bd_sb[:, 0, j, :], xt_[:, :])], nc)
        Ib = next_bank()
        sc.mm(pe, [b_bd, xb_], Ib, [(bd_sb[:, 1, j, :], xt_[:, :])], nc)
        return Rb, Ib

    def late_a1(l, t, j, Rb, Ib):
        Rr, Ii = wr(t, R_R), wr(t, R_I)
        sc.op(act, [Rb[0], b_der], [Rr[0]], lambda: nc.scalar.activation(
            out=Rr[1][:, 0:N], in_=Rb[1], func=AF.Tanh, bias=dcol(l, 0, j), scale=0.5))
        sc.op(act, [Ib[0], b_der], [Ii[0]], lambda: nc.scalar.activation(
            out=Ii[1][:, 0:N], in_=Ib[1], func=AF.Tanh, bias=dcol(l, 1, j), scale=0.5))
        Xb = wr(t, R_X)
        sc.op(dve, [Ii[0], Xb[0]], [Ii[0]], lambda: nc.vector.scalar_tensor_tensor(
            out=Ii[1][:, 0:N], in0=Ii[1][:, 0:N], scalar=1.0, in1=Xb[1][:, 0:N], op0=ALU.add, op1=ALU.mult))

    def late_a(l, t, j):
        Rr, Ii, Tt, Aa, Ee = wr(t, R_R), wr(t, R_I), wr(t, R_T), wr(t, R_A), wr(t, R_E)
        sc.op(act, [Rr[0], b_der], [Tt[0]], lambda: nc.scalar.activation(
            out=Tt[1][:, 0:N], in_=Rr[1][:, 0:N], func=AF.Tanh, bias=dcol(l, 2, j), scale=dcol(l, 2, j)))
        sc.op(act, [Rr[0], b_der], [Aa[0]], lambda: nc.scalar.activation(
            out=Aa[1][:, 0:N], in_=Rr[1][:, 0:N], func=AF.Exp, bias=dcol(l, 3, j), scale=dcol(l, 3, j)))
        sc.op(act, [Rr[0], b_der], [Ee[0]], lambda: nc.scalar.activation(
            out=Ee[1][:, 0:N], in_=Rr[1][:, 0:N], func=AF.Exp, bias=dcol(l, 4, j), scale=dcol(l, 4, j)))
        sc.op(dve, [Ee[0], Tt[0]], [Tt[0]], lambda: nc.vector.scalar_tensor_tensor(
            out=Tt[1][:, 0:N], in0=Ee[1][:, 0:N], scalar=1.0, in1=Tt[1][:, 0:N], op0=ALU.add, op1=ALU.mult))

    def late_b0(l, t, j):
        Tt = wr(t, R_T)
        sc.op(act, [Tt[0]], [Tt[0]], lambda: nc.scalar.activation(
            out=Tt[1][:, 0:N], in_=Tt[1][:, 0:N], func=AF.Sqrt, scale=0.25))

    def late_b1(l, t, j):
        Ii, Tt, Aa, Ee = wr(t, R_I), wr(t, R_T), wr(t, R_A), wr(t, R_E)
        Gg = GGB[t][j % 2]
        sc.op(dve, [Ii[0], Tt[0]], [Ii[0]], lambda: nc.vector.tensor_tensor(
            out=Ii[1][:, 0:N], in0=Ii[1][:, 0:N], in1=Tt[1][:, 0:N], op=ALU.mult))
        sc.op(dve, [Aa[0], Ii[0], b_sh[l][j]], [Ee[0]], lambda: nc.vector.tensor_tensor_scan(
            out=Ee[1][:, 0:N], data0=Aa[1][:, 0:N], data1=Ii[1][:, 0:N], initial=st_h[:, l, j:j + 1],
            op0=ALU.mult, op1=ALU.add))
        aux_copy([Ee[0]], [b_sh[l][j]], st_h[:, l, j:j + 1], Ee[1][:, N - 1:N])
        yab, yat = yyb[t * 8 + j]
        aux_mul([Ee[0], Gg[0]], [yab], yat[:, :], Ee[1][:, 0:N], Gg[1])

    def gnorm(l, t, grps=(0, 1)):
        for grp in grps:
            tiles = [yyb[t * 8 + grp * 4 + j] for j in range(4)]
            rsb, rst = norm_stats([(b, tt_[:, :]) for (b, tt_) in tiles], 1.0 / 512)
            gcol = P_GA if grp == 0 else P_GB
            for j in range(4):
                yb_, yt_ = tiles[j]
                sc.op(dve, [yb_, rsb, b_pv], [yb_], lambda yt_=yt_, j=j, gcol=gcol: nc.vector.scalar_tensor_tensor(
                    out=yt_[:, :], in0=yt_[:, :], scalar=pcol(l, gcol + j), in1=rst[:, 0:N],
                    op0=ALU.mult, op1=ALU.mult))

    def outproj(l, t):
        ys = [yyb[t * 8 + k] for k in range(KC)]
        for oc in range(KC):
            bank = next_bank()
            sc.mm(pe, [b_wout] + [y[0] for y in ys], bank,
                  [(wout_sb[:, k * D + oc * 128:k * D + (oc + 1) * 128], ys[k][1][:, :]) for k in range(KC)], nc)
            sc.op(dve, [bank[0], hb[oc][t]], [hb[oc][t]], lambda oc=oc, bank=bank: nc.vector.tensor_tensor(
                out=h_sb[:, oc, tsl(t)], in0=h_sb[:, oc, tsl(t)], in1=bank[1], op=ALU.add))

    def next_win_prefetch(lgi, j):
        q = j + 2
        if q < 4:
            load_win(lgs[lgi][1], q)
        elif lgi + 1 < len(lgs):
            load_win(lgs[lgi + 1][1], q - 4)

    def phase_a(l, lgi):
        nxt = lgs[lgi + 1] if lgi + 1 < len(lgs) else None
        def late_block(jj):
            late_a(l, 0, jj)
            late_a(l, 1, jj)
            late_b0(l, 0, jj)
            late_b0(l, 1, jj)
            late_b1(l, 0, jj)
            late_b1(l, 1, jj)

        for j in range(4):
            k0 = proj(l, 0, j, [0, 1])
            early_conv(l, 0, j, k0[0])
            egate(l, 0, j, k0[1])
            k1 = proj(l, 1, j, [0, 1])
            early_conv(l, 1, j, k1[0])
            egate(l, 1, j, k1[1])
            if j > 0:
                late_block(j - 1)
            b0 = proj(l, 0, j, [2, 3, 4])
            ri0 = ri_mm(l, 0, j)
            late_a1(l, 0, j, *ri0)
            early_b(l, 0, j, b0)
            b1 = proj(l, 1, j, [2, 3, 4])
            next_win_prefetch(lgi, j)
            ri1 = ri_mm(l, 1, j)
            late_a1(l, 1, j, *ri1)
            early_b(l, 1, j, b1)
        late_block(3)
        gnorm(l, 0, (1,))
        gnorm(l, 1, (1,))
        if nxt is not None:
            load_bd(nxt[1])
        gnorm(l, 0, (0,))
        outproj(l, 0)
        gnorm(l, 1, (0,))
        norm_h(l, 0, P_G2)
        outproj(l, 1)
        if nxt is not None:
            load_wout(nxt[1])

    mset = [0]

    def mlp_up(l, c, t, s_, aset):
        for fb in range(4):
            f = c * 4 + fb
            ws = mset[0] % 4
            mset[0] += 1
            Ub, Ab, Uh = MLP_U[ws], MLP_A[ws], MLP_UH[ws]
            bU = next_bank()
            sc.mm(pe, [b_m[s_]] + [vb[k][t] for k in range(KC)], bU,
                  [(mup_sb[s_][:, k * 512 + fb * 128:k * 512 + (fb + 1) * 128], v_sb[:, k, tsl(t)]) for k in range(KC)], nc)
            bGt = next_bank()
            sc.mm(pe, [b_m[s_]] + [vb[k][t] for k in range(KC)], bGt,
                  [(mgt_sb[s_][:, k * 512 + fb * 128:k * 512 + (fb + 1) * 128], v_sb[:, k, tsl(t)]) for k in range(KC)], nc)
            sc.op(act, [bU[0]], [Ub[0]], lambda: nc.scalar.activation(out=Ub[1][:, 2:2 + N], in_=bU[1], func=AF.Copy))
            sc.op(dve, [b_sup[l][f]], [Uh], lambda: nc.vector.tensor_copy(out=Ub[1][:, 0:2], in_=st_up[:, l, f, :]))
            sc.op(act, [Ub[0], b_pv], [Ab[0]], lambda: nc.scalar.activation(
                out=Ab[1][:, 0:N], in_=Ub[1][:, 2:2 + N], func=AF.Identity,
                bias=pcol(l, P_MB + f), scale=pcol(l, P_MW + f * 3 + 2)))
            for k in range(2):
                sc.op(dve, [Ub[0], Uh, Ab[0], b_pv], [Ab[0]], lambda k=k: nc.vector.scalar_tensor_tensor(
                    out=Ab[1][:, 0:N], in0=Ub[1][:, k:k + N], scalar=pcol(l, P_MW + f * 3 + k), in1=Ab[1][:, 0:N],
                    op0=ALU.mult, op1=ALU.add))
            sc.op(dve, [Ub[0]], [b_sup[l][f]], lambda: nc.vector.tensor_copy(out=st_up[:, l, f, :], in_=Ub[1][:, N:N + 2]))
            sc.op(act, [Ab[0]], [Ab[0]], lambda: nc.scalar.activation(
                out=Ab[1][:, 0:N], in_=Ab[1][:, 0:N], func=AF.Gelu_apprx_tanh))
            ab_, at_ = yyb[aset * 4 + fb]
            sc.op(dve, [Ab[0], bGt[0]], [ab_], lambda: nc.vector.tensor_tensor(
                out=at_[:, :], in0=Ab[1][:, 0:N], in1=bGt[1], op=ALU.mult))

    def mlp_down(l, c, t, s_, aset):
        acts = [yyb[aset * 4 + fb] for fb in range(4)]
        for oc in range(KC):
            bank = next_bank()
            sc.mm(pe, [b_md[s_]] + [a[0] for a in acts], bank,
                  [(mdn_sb[s_][:, fb * D + oc * 128:fb * D + (oc + 1) * 128], acts[fb][1][:, :]) for fb in range(4)], nc)
            sc.op(dve, [bank[0], hb[oc][t]], [hb[oc][t]], lambda oc=oc, bank=bank: nc.vector.tensor_tensor(
                out=h_sb[:, oc, tsl(t)], in0=h_sb[:, oc, tsl(t)], in1=bank[1], op=ALU.add))

    def final_tile(seq, half, t):
        srcs = [(hb[k][t], h_sb[:, k, tsl(t)]) for k in range(KC)]
        rsb, rst = norm_stats(srcs, 1.0 / D)
        for k in range(KC):
            ob, ot = STAGE[k % 6]
            sc.op(dve, [hb[k][t], rsb, b_pv], [ob], lambda k=k, ot=ot: nc.vector.scalar_tensor_tensor(
                out=ot[:, 0:N], in0=h_sb[:, k, tsl(t)], scalar=pcol(0, P_FG + k), in1=rst[:, 0:N],
                op0=ALU.mult, op1=ALU.mult))
            sc.dma(sp, c_out, [ob], [], lambda k=k, ot=ot: nc.sync.dma_start(
                out=yT[seq, k * 128:(k + 1) * 128, half * G + t * N: half * G + (t + 1) * N], in_=ot[:, 0:N]))

    for lgi, (g, l) in enumerate(lgs):
        seq, half = divmod(g, S // G)
        last_layer = (l == nlayers - 1)
        if l == 0:
            for k in range(KC):
                sc.dma(sp, c_x, [], [hb[k][t] for t in range(NT)], lambda k=k: nc.sync.dma_start(
                    out=h_sb[:, k, :], in_=xT[seq, k * 128:(k + 1) * 128, half * G:(half + 1) * G]))
            if half == 0:
                for ll in range(nlayers):
                    sc.op(dve, [], b_slx[ll], lambda ll=ll: nc.vector.memset(st_lx[:, ll, :, :], 0.0))
                    sc.op(dve, [], b_sh[ll], lambda ll=ll: nc.vector.memset(st_h[:, ll, :], 0.0))
                    sc.op(dve, [], b_scv[ll], lambda ll=ll: nc.vector.memset(st_cv[:, ll, :, :], 0.0))
                    sc.op(dve, [], b_sup[ll], lambda ll=ll: nc.vector.memset(st_up[:, ll, :, :], 0.0))
            norm_h(l, 0, P_G1)
            norm_h(l, 1, P_G1)
        phase_a(l, lgi)
        units = [(c, t) for c in range(6) for t in range(NT)]
        prev = None
        for ui, (c, t) in enumerate(units):
            s_ = c % 2
            aset = ui % 2
            mlp_up(l, c, t, s_, aset)
            if t == NT - 1:
                _prefetch_mlp(sc, lgs, lgi, c, s_, load_mlp, 0)
            if ui == 0:
                norm_h(l, 1, P_G2)
            if prev is not None:
                pc, pt, ps_, pa = prev
                mlp_down(l, pc, pt, ps_, pa)
                if pt == NT - 1:
                    _prefetch_mlp(sc, lgs, lgi, pc, ps_, load_mlp, 1)
            prev = (c, t, s_, aset)
        pc, pt, ps_, pa = prev
        if last_layer:
            final_tile(seq, half, 0)
        else:
            norm_h(l + 1, 0, P_G1)
        mlp_down(l, pc, pt, ps_, pa)
        _prefetch_mlp(sc, lgs, lgi, pc, ps_, load_mlp, 1)
        if last_layer:
            final_tile(seq, half, 1)
        else:
            norm_h(l + 1, 1, P_G1)
    nc.sync.wait_ge(c_out.sem, c_out.count)
    es.close()
    return nc


def _prefetch_mlp(sc, lgs, lgi, c, s_, load_mlp, which):
    nc_ = c + 2
    if nc_ < 6:
        load_mlp(lgs[lgi][1], nc_, s_, which)
    elif lgi + 1 < len(lgs):
        load_mlp(lgs[lgi + 1][1], nc_ - 6, s_, which)


def pack_params(inp):
    pv = np.zeros((128, L, NPV), np.float32)

    def ch(v, n):
        return np.ascontiguousarray(v.reshape(n, 128).T)
    for l in range(L):
        pv[:, l, P_G1:P_G1 + 8] = ch(inp["norm1_g"][l], 8)
        cw = inp["lru_conv_w"][l]
        for j in range(4):
            for k in range(4):
                pv[:, l, P_CW + j * 4 + k] = cw[k, j * 128:(j + 1) * 128]
        pv[:, l, P_CB:P_CB + 4] = ch(inp["lru_conv_b"][l], 4)
        pv[:, l, P_BA:P_BA + 4] = ch(inp["lru_b_a"][l].reshape(512), 4)
        pv[:, l, P_BX:P_BX + 4] = ch(inp["lru_b_x"][l].reshape(512), 4)
        pv[:, l, P_LAM:P_LAM + 4] = ch(inp["lru_lambda"][l], 4)
        sw = inp["sc_conv_w"][l]
        for j in range(4):
            for k in range(3):
                pv[:, l, P_SW + j * 3 + k] = sw[k, j * 128:(j + 1) * 128]
        pv[:, l, P_GA:P_GA + 4] = ch(inp["lru_out_g"][l], 4)
        pv[:, l, P_GB:P_GB + 4] = ch(inp["sc_out_g"][l], 4)
        pv[:, l, P_G2:P_G2 + 8] = ch(inp["norm2_g"][l], 8)
        mw = inp["mlp_conv_w"][l]
        for f in range(24):
            for k in range(3):
                pv[:, l, P_MW + f * 3 + k] = mw[k, f * 128:(f + 1) * 128]
        pv[:, l, P_MB:P_MB + 24] = ch(inp["mlp_conv_b"][l], 24)
        pv[:, l, P_FG:P_FG + 8] = ch(inp["final_g"], 8)
    return pv


def permute_w_in(w):
    out = np.empty_like(w)
    for j in range(4):
        for bi, off in enumerate(BLK_OFFS):
            out[:, :, j * 640 + bi * 128: j * 640 + (bi + 1) * 128] = w[:, :, off + j * 128: off + (j + 1) * 128]
    return out


def relayout_weights(inp):
    f = lambda a: np.asarray(a, dtype=np.float32)
    w_in_p = permute_w_in(f(inp["w_in"]))
    return {
        "w_in": np.ascontiguousarray(w_in_p.reshape(L, 8, 128, 4, 640).transpose(0, 3, 2, 1, 4)).reshape(L, 4, 128, 5120),
        "w_out": np.ascontiguousarray(f(inp["w_out"]).reshape(L, 8, 128, 1024).transpose(0, 2, 1, 3)).reshape(L, 128, 8192),
        "mlp_w_up": np.ascontiguousarray(f(inp["mlp_w_up"]).reshape(L, 8, 128, 6, 512).transpose(0, 3, 2, 1, 4)).reshape(L, 6, 128, 4096),
        "mlp_w_gate": np.ascontiguousarray(f(inp["mlp_w_gate"]).reshape(L, 8, 128, 6, 512).transpose(0, 3, 2, 1, 4)).reshape(L, 6, 128, 4096),
        "mlp_w_down": np.ascontiguousarray(f(inp["mlp_w_down"]).reshape(L, 6, 4, 128, 1024).transpose(0, 1, 3, 2, 4)).reshape(L, 6, 128, 4096),
        "lru_w_a": np.ascontiguousarray(f(inp["lru_w_a"])),
        "lru_w_x": np.ascontiguousarray(f(inp["lru_w_x"])),
    }


_NC_CACHE = {}


def kernel(**inp):
    inp = {k: np.asarray(v) for k, v in inp.items()}
    x = inp["x"].astype(np.float32, copy=False)
    if "full" not in _NC_CACHE:
        _NC_CACHE["full"] = build_nc()
    nc = _NC_CACHE["full"]
    pv = pack_params(inp)
    shared = relayout_weights(inp)
    shared["pvec"] = pv
    in_maps = []
    for c in range(NCORES):
        xs = x[c * SPC:(c + 1) * SPC]
        m = dict(shared)
        m["xT"] = np.ascontiguousarray(xs.transpose(0, 2, 1))
        in_maps.append(m)
    res = run_bass_kernel_spmd(nc, in_maps, core_ids=list(range(NCORES)))
    out = np.empty((NCORES * SPC, S, D), np.float32)
    for c in range(NCORES):
        out[c * SPC:(c + 1) * SPC] = res.results[c]["yT"].transpose(0, 2, 1)
    return out
```

```python
import numpy as np
from contextlib import ExitStack
import concourse.bass as bass
import concourse.mybir as mybir
from concourse.bass_utils import run_bass_kernel_spmd

F32 = mybir.dt.float32
BF16 = mybir.dt.bfloat16
AF = mybir.ActivationFunctionType
ALU = mybir.AluOpType

D = 1024
S = 2048
L = 4
DFF = 3072
NCORES = 8
SPC = 2
G = 1024
N = 512
NT = G // N
KC = D // 128
HW = 516
EPS = 1e-6

P_G1, P_CW, P_CB, P_BA, P_BX, P_LAM, P_SW, P_GA, P_GB, P_G2, P_MW, P_MB, P_FG, NPV = (
    0, 8, 24, 28, 32, 36, 40, 52, 56, 60, 68, 140, 164, 172)


class Eng:
    def __init__(self, name, h, sem, is_dma=False):
        self.name, self.h, self.sem, self.is_dma = name, h, sem, is_dma
        self.count = 0
        self.waited = {}


class Buf:
    __slots__ = ("name", "w", "r")

    def __init__(self, name):
        self.name = name
        self.w = None
        self.r = {}


class Sched:
    def _deps(self, reads, writes):
        deps = {}

        def add(e, t):
            if e.is_dma:
                t = e.count
            if t > deps.get(e, 0):
                deps[e] = t
        for b in reads:
            if b.w is not None:
                add(*b.w)
        for b in writes:
            if b.w is not None:
                add(*b.w)
            for e, t in b.r.items():
                add(e, t)
        return deps

    def _wait(self, eng, deps):
        for e, t in deps.items():
            if e is eng and eng.name == "pe":
                continue
            if t > eng.waited.get(e, 0):
                eng.h.wait_ge(e.sem, t)
                eng.waited[e] = t

    def _mark(self, eng, tk, reads, writes):
        for b in reads:
            if tk > b.r.get(eng, 0):
                b.r[eng] = tk
        for b in writes:
            b.w = (eng, tk)
            b.r = {}

    def op(self, eng, reads, writes, fn):
        self._wait(eng, self._deps(reads, writes))
        ins = fn()
        eng.count += 1
        ins.then_inc(eng.sem, 1)
        self._mark(eng, eng.count, reads, writes)

    def mm(self, pe, reads, bank, mms, nc):
        self._wait(pe, self._deps(reads, [bank[0]]))
        n = len(mms)
        ins = None
        for i, (lhsT, rhs) in enumerate(mms):
            ins = nc.tensor.matmul(bank[1], lhsT, rhs, start=(i == 0), stop=(i == n - 1))
        pe.count += 1
        ins.then_inc(pe.sem, 1)
        self._mark(pe, pe.count, reads, [bank[0]])

    def dma(self, q, chan, reads, writes, fn):
        self._wait(q, self._deps(reads, writes))
        ins = fn()
        chan.count += 16
        ins.then_inc(chan.sem, 16)
        self._mark(chan, chan.count, reads, writes)


BLK_OFFS = (0, 512, 1536, 2048, 1024)


def build_nc(nlayers=L, ngroups=SPC * (S // G), pool_offload=True):
    nc = bass.Bass("TRN2", target_bir_lowering=False)
    es = ExitStack()

    def dram(name, shape, kind="ExternalInput"):
        return nc.dram_tensor(name, shape, F32, kind=kind).ap()

    xT = dram("xT", [SPC, D, S])
    pvec = dram("pvec", [128, L, NPV])
    w_in = dram("w_in", [L, 4, 128, 5120])
    w_a = dram("lru_w_a", [L, 8, 64, 64])
    w_x = dram("lru_w_x", [L, 8, 64, 64])
    w_out = dram("w_out", [L, 128, 8192])
    w_up = dram("mlp_w_up", [L, 6, 128, 4096])
    w_gate = dram("mlp_w_gate", [L, 6, 128, 4096])
    w_down = dram("mlp_w_down", [L, 6, 128, 4096])
    yT = dram("yT", [SPC, D, S], kind="ExternalOutput")

    def sb(name, shape, dt=F32):
        return es.enter_context(nc.sbuf_tensor(name, shape, dt))

    def sem(name):
        return es.enter_context(nc.semaphore(name))

    pe = Eng("pe", nc.tensor, sem("s_pe"))
    act = Eng("act", nc.scalar, sem("s_act"))
    dve = Eng("dve", nc.vector, sem("s_dve"))
    pool = Eng("pool", nc.gpsimd, sem("s_pool"))
    sp = Eng("sp", nc.sync, sem("s_sp"))
    aux = pool if pool_offload else dve

    def chan(name):
        return Eng(name, None, sem("c_" + name), is_dma=True)
    sc = Sched()

    h_sb = sb("h", [128, KC, G])
    v_sb = sb("v", [128, KC, G], BF16)
    win_sb = [sb(f"win{i}", [128, KC * 640], BF16) for i in range(2)]
    wout_sb = sb("wout", [128, KC * D], BF16)
    bd_sb = sb("bd", [128, 2, 4, 128], BF16)
    mup_sb = [sb(f"mup{i}", [128, KC * 512], BF16) for i in range(2)]
    mgt_sb = [sb(f"mgt{i}", [128, KC * 512], BF16) for i in range(2)]
    mdn_sb = [sb(f"mdn{i}", [128, 4 * D], BF16) for i in range(2)]
    pv_sb = sb("pv", [128, L, NPV])
    der_sb = sb("der", [128, L, 5, 4])
    tmp_sb = sb("tmpc", [128, L, 4])
    cst_sb = sb("cst", [128, 4])
    ones_sb = sb("ones", [128, 128], BF16)
    st_lx = sb("st_lx", [128, L, 4, 3])
    st_h = sb("st_h", [128, L, 4])
    st_cv = sb("st_cv", [128, L, 4, 2])
    st_up = sb("st_up", [128, L, 24, 2])
    NW = 22
    wk_sb = [sb(f"wk{i}", [128, HW]) for i in range(NW)]
    yb_sb = [sb(f"yy{i}", [128, N], BF16) for i in range(16)]
    xabf_sb = [sb(f"xabf{i}", [128, N], BF16) for i in range(2)]
    sq_sb = [sb(f"sq{i}", [128, N], BF16) for i in range(2)]
    ps_t = [es.enter_context(nc.psum_tensor(f"ps{i}", [128, N], F32)) for i in range(8)]

    hb = [[Buf(f"h{k}_{t}") for t in range(NT)] for k in range(KC)]
    vb = [[Buf(f"v{k}_{t}") for t in range(NT)] for k in range(KC)]
    b_win = [Buf("win0"), Buf("win1")]
    b_wout, b_bd = Buf("wout"), Buf("bd")
    b_m = [Buf("m0"), Buf("m1")]
    b_md = [Buf("md0"), Buf("md1")]
    b_pv, b_der, b_tmp, b_cst, b_ones = Buf("pv"), Buf("der"), Buf("tmp"), Buf("cst"), Buf("ones")
    b_slx = [[Buf(f"slx{l}_{j}") for j in range(4)] for l in range(L)]
    b_sh = [[Buf(f"sh{l}_{j}") for j in range(4)] for l in range(L)]
    b_scv = [[Buf(f"scv{l}_{j}") for j in range(4)] for l in range(L)]
    b_sup = [[Buf(f"sup{l}_{f}") for f in range(24)] for l in range(L)]
    wk = [(Buf(f"wk{i}"), wk_sb[i]) for i in range(NW)]
    wh = [Buf(f"wh{i}") for i in range(NW)]
    yyb = [(Buf(f"yy{i}"), yb_sb[i]) for i in range(16)]
    xabf = [(Buf(f"xabf{i}"), xabf_sb[i]) for i in range(2)]
    sqs = [(Buf(f"sq{i}"), sq_sb[i]) for i in range(2)]
    banks = [(Buf(f"ps{i}"), ps_t[i][:, :]) for i in range(8)]
    bank_i = [0]

    def next_bank():
        b = banks[bank_i[0] % 8]
        bank_i[0] += 1
        return b
    sq_i = [0]

    def next_sq():
        s_ = sqs[sq_i[0] % 2]
        sq_i[0] += 1
        return s_

    R_L, R_X, R_R, R_I, R_T, R_A, R_E, R_G, R_C, R_V = range(10)

    def wr(p, r):
        return wk[p * 10 + r]
    SD, RS = wk[20], wk[21]
    def whb(p, r):
        return wh[p * 10 + r]
    MLP_U = [wr(0, R_L), wr(0, R_V), wr(1, R_L), wr(1, R_V)]
    MLP_UH = [whb(0, R_L), whb(0, R_V), whb(1, R_L), whb(1, R_V)]
    MLP_A = [wr(0, R_X), wr(0, R_R), wr(1, R_X), wr(1, R_R)]
    STAGE = [wr(0, R_T), wr(0, R_A), wr(0, R_E), wr(1, R_T), wr(1, R_A), wr(1, R_E)]
    GGB = [[(Buf(f"gg{t}_{p}"), wk_sb[t * 10 + R_G].bitcast(BF16)[:, p * HW:p * HW + N]) for p in range(2)]
           for t in range(2)]

    c_x, c_pv, c_wout, c_bd = chan("x"), chan("pv"), chan("wout"), chan("bd")
    c_win = [chan("win0"), chan("win1")]
    c_m = [chan("m0"), chan("m1")]
    c_md = [chan("md0"), chan("md1")]
    c_out = chan("out")

    def pcol(l, c):
        return pv_sb[:, l, c:c + 1]

    def dcol(l, i, j):
        return der_sb[:, l, i, j:j + 1]

    def tsl(t):
        return slice(t * N, (t + 1) * N)

    def big_dma(ch, buf, dst, src, b):
        sc.dma(pool, ch, [], [buf], lambda: nc.gpsimd.dma_start(
            out=dst[:, :].rearrange("p (a b) -> p a b", b=b), in_=src.rearrange("p (a b) -> p a b", b=b)))

    def load_win(l, j):
        s_ = j % 2
        big_dma(c_win[s_], b_win[s_], win_sb[s_], w_in[l, j], 1280)

    def load_wout(l):
        big_dma(c_wout, b_wout, wout_sb, w_out[l], 2048)

    def load_bd(l):
        for gi, wsrc in enumerate((w_a, w_x)):
            src = wsrc[l].rearrange("(j two) d e -> two d j e", two=2)
            for two in range(2):
                sc.dma(pool, c_bd, [], [b_bd], lambda gi=gi, two=two, src=src: nc.gpsimd.dma_start(
                    out=bd_sb[two * 64:(two + 1) * 64, gi, :, two * 64:(two + 1) * 64], in_=src[two]))

    def load_mlp(l, c, s_, which):
        if which == 0:
            big_dma(c_m[s_], b_m[s_], mup_sb[s_], w_up[l, c], 2048)
            big_dma(c_m[s_], b_m[s_], mgt_sb[s_], w_gate[l, c], 2048)
        else:
            big_dma(c_md[s_], b_md[s_], mdn_sb[s_], w_down[l, c], 2048)

    sc.dma(sp, c_pv, [], [b_pv], lambda: nc.sync.dma_start(out=pv_sb[:, :, :], in_=pvec))
    sc.op(dve, [], [b_cst], lambda: nc.vector.memset(cst_sb[:, 0:1], EPS))
    sc.op(dve, [], [b_cst], lambda: nc.vector.memset(cst_sb[:, 1:2], 1.0))
    sc.op(dve, [], [b_ones], lambda: nc.vector.memset(ones_sb[:, :], 1.0))
    sc.op(pool, [], [b_bd], lambda: nc.gpsimd.memset(bd_sb[:, :, :, :], 0.0))
    lam = pv_sb[:, :, P_LAM:P_LAM + 4]
    sc.op(act, [b_pv], [b_tmp], lambda: nc.scalar.activation(out=tmp_sb[:, :, :], in_=lam, func=AF.Exp, scale=-1.0))
    sc.op(act, [b_tmp, b_cst], [b_tmp], lambda: nc.scalar.activation(
        out=tmp_sb[:, :, :], in_=tmp_sb[:, :, :], func=AF.Ln, bias=cst_sb[:, 1:2], scale=1.0))
    sc.op(dve, [b_pv], [b_der], lambda: nc.vector.tensor_scalar(
        out=der_sb[:, :, 0, :], in0=pv_sb[:, :, P_BA:P_BA + 4], scalar1=0.5, scalar2=None, op0=ALU.mult))
    sc.op(dve, [b_pv], [b_der], lambda: nc.vector.tensor_scalar(
        out=der_sb[:, :, 1, :], in0=pv_sb[:, :, P_BX:P_BX + 4], scalar1=0.5, scalar2=None, op0=ALU.mult))
    for idx, mul in ((2, 4.0), (3, -4.0), (4, -8.0)):
        sc.op(dve, [b_tmp], [b_der], lambda idx=idx, mul=mul: nc.vector.tensor_scalar(
            out=der_sb[:, :, idx, :], in0=tmp_sb[:, :, :], scalar1=mul, scalar2=None, op0=ALU.mult))

    lgs = [(g, l) for g in range(ngroups) for l in range(nlayers)]
    load_win(0, 0)
    load_win(0, 1)
    load_bd(0)
    load_wout(0)
    load_mlp(0, 0, 0, 0)
    load_mlp(0, 0, 0, 1)
    load_mlp(0, 1, 1, 0)
    load_mlp(0, 1, 1, 1)

    def aux_copy(reads, writes, out, in_):
        if False:
            sc.op(pool, reads, writes, lambda: nc.gpsimd.tensor_copy(out=out, in_=in_))
        else:
            sc.op(dve, reads, writes, lambda: nc.vector.tensor_copy(out=out, in_=in_))

    def aux_mul(reads, writes, out, in0, in1):
        if False:
            sc.op(pool, reads, writes, lambda: nc.gpsimd.tensor_tensor(out=out, in0=in0, in1=in1, op=ALU.mult))
        else:
            sc.op(dve, reads, writes, lambda: nc.vector.tensor_tensor(out=out, in0=in0, in1=in1, op=ALU.mult))

    def norm_parts(srcs, scale):
        bank = next_bank()
        n = len(srcs)

        def step(i):
            b, ap = srcs[i]
            sqb, sqt = next_sq()
            sc.op(act, [b], [sqb], lambda: nc.scalar.activation(out=sqt[:, :], in_=ap, func=AF.Square))
            sc._wait(pe, sc._deps([b_ones, sqb], [bank[0]]))
            ins = nc.tensor.matmul(bank[1], ones_sb[:, :], sqt[:, :], start=(i == 0), stop=(i == n - 1))
            pe.count += 1
            ins.then_inc(pe.sem, 1)
            sc._mark(pe, pe.count, [b_ones, sqb], [bank[0]])

        def finish():
            sc.op(act, [bank[0], b_cst], [SD[0]], lambda: nc.scalar.activation(
                out=SD[1][:, 0:N], in_=bank[1], func=AF.Ln, bias=cst_sb[:, 0:1], scale=scale))
            sc.op(act, [SD[0]], [RS[0]], lambda: nc.scalar.activation(
                out=RS[1][:, 0:N], in_=SD[1][:, 0:N], func=AF.Exp, scale=-0.5))
            return RS
        return step, finish

    def norm_stats(srcs, scale):
        step, finish = norm_parts(srcs, scale)
        for i in range(len(srcs)):
            step(i)
        return finish()

    def norm_apply(l, t, gcol, rsb, rst):
        for k in range(KC):
            sc.op(dve, [hb[k][t], rsb, b_pv], [vb[k][t]], lambda k=k: nc.vector.scalar_tensor_tensor(
                out=v_sb[:, k, tsl(t)], in0=h_sb[:, k, tsl(t)], scalar=pcol(l, gcol + k), in1=rst[:, 0:N],
                op0=ALU.mult, op1=ALU.mult))

    def norm_h(l, t, gcol):
        srcs = [(hb[k][t], h_sb[:, k, tsl(t)]) for k in range(KC)]
        rsb, rst = norm_stats(srcs, 1.0 / D)
        for k in range(KC):
            sc.op(dve, [hb[k][t], rsb, b_pv], [vb[k][t]], lambda k=k: nc.vector.scalar_tensor_tensor(
                out=v_sb[:, k, tsl(t)], in0=h_sb[:, k, tsl(t)], scalar=pcol(l, gcol + k), in1=rst[:, 0:N],
                op0=ALU.mult, op1=ALU.mult))

    def proj(l, t, j, blks=range(5)):
        bks = []
        s_ = j % 2
        for blk in blks:
            bank = next_bank()
            col = blk * 128
            sc.mm(pe, [b_win[s_]] + [vb[k][t] for k in range(KC)], bank,
                  [(win_sb[s_][:, k * 640 + col:k * 640 + col + 128], v_sb[:, k, tsl(t)]) for k in range(KC)], nc)
            bks.append(bank)
        return bks

    def early_conv(l, t, j, bA):
        Lb, Xb = wr(t, R_L), wr(t, R_X)
        xb_, xt_ = xabf[t]
        sc.op(act, [bA[0]], [Lb[0]], lambda: nc.scalar.activation(out=Lb[1][:, 3:3 + N], in_=bA[1], func=AF.Copy))
        Lh = whb(t, R_L)
        aux_copy([b_slx[l][j]], [Lh], Lb[1][:, 0:3], st_lx[:, l, j, :])
        sc.op(act, [Lb[0], b_pv], [Xb[0]], lambda: nc.scalar.activation(
            out=Xb[1][:, 0:N], in_=Lb[1][:, 3:3 + N], func=AF.Identity,
            bias=pcol(l, P_CB + j), scale=pcol(l, P_CW + j * 4 + 3)))
        for k in range(3):
            sc.op(dve, [Lb[0], Lh, Xb[0], b_pv], [Xb[0]], lambda k=k: nc.vector.scalar_tensor_tensor(
                out=Xb[1][:, 0:N], in0=Lb[1][:, k:k + N], scalar=pcol(l, P_CW + j * 4 + k), in1=Xb[1][:, 0:N],
                op0=ALU.mult, op1=ALU.add))
        aux_copy([Lb[0]], [b_slx[l][j]], st_lx[:, l, j, :], Lb[1][:, N:N + 3])
        sc.op(act, [Xb[0]], [xb_], lambda: nc.scalar.activation(out=xt_[:, :], in_=Xb[1][:, 0:N], func=AF.Copy))

    def egate(l, t, j, bG):
        Gg = GGB[t][j % 2]
        sc.op(act, [bG[0]], [Gg[0]], lambda: nc.scalar.activation(out=Gg[1], in_=bG[1], func=AF.Gelu_apprx_tanh))

    def early_b(l, t, j, bks):
        bC, bV, bB = bks
        C1, Vb = wr(t, R_C), wr(t, R_V)
        Vh = whb(t, R_V)
        sc.op(act, [bC[0]], [C1[0]], lambda: nc.scalar.activation(out=C1[1][:, 0:N], in_=bC[1], func=AF.Copy))
        sc.op(dve, [C1[0], bV[0]], [Vb[0]], lambda: nc.vector.tensor_tensor(
            out=Vb[1][:, 2:2 + N], in0=C1[1][:, 0:N], in1=bV[1], op=ALU.mult))
        aux_copy([b_scv[l][j]], [Vh], Vb[1][:, 0:2], st_cv[:, l, j, :])
        sc.op(act, [Vb[0], b_pv], [C1[0]], lambda: nc.scalar.activation(
            out=C1[1][:, 0:N], in_=Vb[1][:, 2:2 + N], func=AF.Identity, scale=pcol(l, P_SW + j * 3 + 2)))
        for k in range(2):
            sc.op(dve, [Vb[0], Vh, C1[0], b_pv], [C1[0]], lambda k=k: nc.vector.scalar_tensor_tensor(
                out=C1[1][:, 0:N], in0=Vb[1][:, k:k + N], scalar=pcol(l, P_SW + j * 3 + k), in1=C1[1][:, 0:N],
                op0=ALU.mult, op1=ALU.add))
        aux_copy([Vb[0]], [b_scv[l][j]], st_cv[:, l, j, :], Vb[1][:, N:N + 2])
        ybb, ybt = yyb[t * 8 + 4 + j]
        sc.op(dve, [C1[0], bB[0]], [ybb], lambda: nc.vector.tensor_tensor(
            out=ybt[:, :], in0=C1[1][:, 0:N], in1=bB[1], op=ALU.mult))

    def ri_mm(l, t, j):
        xb_, xt_ = xabf[t]
        Rb = next_bank()
        sc.mm(pe, [b_bd, xb_], Rb, [(bd_sb[:, 0, j, :], xt_[:, :])], nc)
        Ib = next_bank()
        sc.mm(pe, [b_bd, xb_], Ib, [(bd_sb[:, 1, j, :], xt_[:, :])], nc)
        return Rb, Ib

    def late_a1(l, t, j, Rb, Ib):
        Rr, Ii = wr(t, R_R), wr(t, R_I)
        sc.op(act, [Rb[0], b_der], [Rr[0]], lambda: nc.scalar.activation(
            out=Rr[1][:, 0:N], in_=Rb[1], func=AF.Tanh, bias=dcol(l, 0, j), scale=0.5))
        sc.op(act, [Ib[0], b_der], [Ii[0]], lambda: nc.scalar.activation(
            out=Ii[1][:, 0:N], in_=Ib[1], func=AF.Tanh, bias=dcol(l, 1, j), scale=0.5))
        Xb = wr(t, R_X)
        sc.op(dve, [Ii[0], Xb[0]], [Ii[0]], lambda: nc.vector.scalar_tensor_tensor(
            out=Ii[1][:, 0:N], in0=Ii[1][:, 0:N], scalar=1.0, in1=Xb[1][:, 0:N], op0=ALU.add, op1=ALU.mult))

    def late_a(l, t, j):
        Rr, Ii, Tt, Aa, Ee = wr(t, R_R), wr(t, R_I), wr(t, R_T), wr(t, R_A), wr(t, R_E)
        sc.op(act, [Rr[0], b_der], [Tt[0]], lambda: nc.scalar.activation(
            out=Tt[1][:, 0:N], in_=Rr[1][:, 0:N], func=AF.Tanh, bias=dcol(l, 2, j), scale=dcol(l, 2, j)))
        sc.op(act, [Rr[0], b_der], [Aa[0]], lambda: nc.scalar.activation(
            out=Aa[1][:, 0:N], in_=Rr[1][:, 0:N], func=AF.Exp, bias=dcol(l, 3, j), scale=dcol(l, 3, j)))
        sc.op(act, [Rr[0], b_der], [Ee[0]], lambda: nc.scalar.activation(
            out=Ee[1][:, 0:N], in_=Rr[1][:, 0:N], func=AF.Exp, bias=dcol(l, 4, j), scale=dcol(l, 4, j)))
        sc.op(dve, [Ee[0], Tt[0]], [Tt[0]], lambda: nc.vector.scalar_tensor_tensor(
            out=Tt[1][:, 0:N], in0=Ee[1][:, 0:N], scalar=1.0, in1=Tt[1][:, 0:N], op0=ALU.add, op1=ALU.mult))

    def late_b0(l, t, j):
        Tt = wr(t, R_T)
        sc.op(act, [Tt[0]], [Tt[0]], lambda: nc.scalar.activation(
            out=Tt[1][:, 0:N], in_=Tt[1][:, 0:N], func=AF.Sqrt, scale=0.25))

    def late_b1(l, t, j):
        Ii, Tt, Aa, Ee = wr(t, R_I), wr(t, R_T), wr(t, R_A), wr(t, R_E)
        Gg = GGB[t][j % 2]
        sc.op(dve, [Ii[0], Tt[0]], [Ii[0]], lambda: nc.vector.tensor_tensor(
            out=Ii[1][:, 0:N], in0=Ii[1][:, 0:N], in1=Tt[1][:, 0:N], op=ALU.mult))
        sc.op(dve, [Aa[0], Ii[0], b_sh[l][j]], [Ee[0]], lambda: nc.vector.tensor_tensor_scan(
            out=Ee[1][:, 0:N], data0=Aa[1][:, 0:N], data1=Ii[1][:, 0:N], initial=st_h[:, l, j:j + 1],
            op0=ALU.mult, op1=ALU.add))
        aux_copy([Ee[0]], [b_sh[l][j]], st_h[:, l, j:j + 1], Ee[1][:, N - 1:N])
        yab, yat = yyb[t * 8 + j]
        aux_mul([Ee[0], Gg[0]], [yab], yat[:, :], Ee[1][:, 0:N], Gg[1])

    def gnorm(l, t, grps=(0, 1)):
        for grp in grps:
            tiles = [yyb[t * 8 + grp * 4 + j] for j in range(4)]
            rsb, rst = norm_stats([(b, tt_[:, :]) for (b, tt_) in tiles], 1.0 / 512)
            gcol = P_GA if grp == 0 else P_GB
            for j in range(4):
                yb_, yt_ = tiles[j]
                sc.op(dve, [yb_, rsb, b_pv], [yb_], lambda yt_=yt_, j=j, gcol=gcol: nc.vector.scalar_tensor_tensor(
                    out=yt_[:, :], in0=yt_[:, :], scalar=pcol(l, gcol + j), in1=rst[:, 0:N],
                    op0=ALU.mult, op1=ALU.mult))

    def outproj(l, t, ocs=range(KC)):
        ys = [yyb[t * 8 + k] for k in range(KC)]
        for oc in ocs:
            bank = next_bank()
            sc.mm(pe, [b_wout] + [y[0] for y in ys], bank,
                  [(wout_sb[:, k * D + oc * 128:k * D + (oc + 1) * 128], ys[k][1][:, :]) for k in range(KC)], nc)
            sc.op(dve, [bank[0], hb[oc][t]], [hb[oc][t]], lambda oc=oc, bank=bank: nc.vector.tensor_tensor(
                out=h_sb[:, oc, tsl(t)], in0=h_sb[:, oc, tsl(t)], in1=bank[1], op=ALU.add))

    def next_win_prefetch(lgi, j):
        q = j + 2
        if q < 4:
            load_win(lgs[lgi][1], q)
        elif lgi + 1 < len(lgs):
            load_win(lgs[lgi + 1][1], q - 4)

    def phase_a(l, lgi):
        nxt = lgs[lgi + 1] if lgi + 1 < len(lgs) else None
        def late_block(jj):
            late_a(l, 0, jj)
            late_a(l, 1, jj)
            late_b0(l, 0, jj)
            late_b0(l, 1, jj)
            late_b1(l, 0, jj)
            late_b1(l, 1, jj)

        for j in range(4):
            k0 = proj(l, 0, j, [0, 1])
            early_conv(l, 0, j, k0[0])
            egate(l, 0, j, k0[1])
            k1 = proj(l, 1, j, [0, 1])
            early_conv(l, 1, j, k1[0])
            egate(l, 1, j, k1[1])
            if j > 0:
                late_block(j - 1)
            b0 = proj(l, 0, j, [2, 3, 4])
            ri0 = ri_mm(l, 0, j)
            late_a1(l, 0, j, *ri0)
            early_b(l, 0, j, b0)
            b1 = proj(l, 1, j, [2, 3, 4])
            next_win_prefetch(lgi, j)
            ri1 = ri_mm(l, 1, j)
            late_a1(l, 1, j, *ri1)
            early_b(l, 1, j, b1)
        late_block(3)
        gnorm(l, 0, (1,))
        gnorm(l, 1, (1,))
        if nxt is not None:
            load_bd(nxt[1])
        gnorm(l, 0, (0,))
        gnorm(l, 1, (0,))
        outproj(l, 0)
        step, fin = norm_parts([(hb[k][0], h_sb[:, k, tsl(0)]) for k in range(KC)], 1.0 / D)
        rs2 = None
        for oc in range(KC):
            outproj(l, 1, [oc])
            if oc < 4:
                step(2 * oc)
                step(2 * oc + 1)
            if oc == 4:
                rs2 = fin()
            if oc == 5:
                norm_apply(l, 0, P_G2, rs2[0], rs2[1])
        if nxt is not None:
            load_wout(nxt[1])

    mset = [0]

    def mlp_up(l, c, t, s_, aset):
        for fb in range(4):
            f = c * 4 + fb
            ws = mset[0] % 4
            mset[0] += 1
            Ub, Ab, Uh = MLP_U[ws], MLP_A[ws], MLP_UH[ws]
            bU = next_bank()
            sc.mm(pe, [b_m[s_]] + [vb[k][t] for k in range(KC)], bU,
                  [(mup_sb[s_][:, k * 512 + fb * 128:k * 512 + (fb + 1) * 128], v_sb[:, k, tsl(t)]) for k in range(KC)], nc)
            bGt = next_bank()
            sc.mm(pe, [b_m[s_]] + [vb[k][t] for k in range(KC)], bGt,
                  [(mgt_sb[s_][:, k * 512 + fb * 128:k * 512 + (fb + 1) * 128], v_sb[:, k, tsl(t)]) for k in range(KC)], nc)
            sc.op(act, [bU[0]], [Ub[0]], lambda: nc.scalar.activation(out=Ub[1][:, 2:2 + N], in_=bU[1], func=AF.Copy))
            sc.op(dve, [b_sup[l][f]], [Uh], lambda: nc.vector.tensor_copy(out=Ub[1][:, 0:2], in_=st_up[:, l, f, :]))
            sc.op(act, [Ub[0], b_pv], [Ab[0]], lambda: nc.scalar.activation(
                out=Ab[1][:, 0:N], in_=Ub[1][:, 2:2 + N], func=AF.Identity,
                bias=pcol(l, P_MB + f), scale=pcol(l, P_MW + f * 3 + 2)))
            for k in range(2):
                sc.op(dve, [Ub[0], Uh, Ab[0], b_pv], [Ab[0]], lambda k=k: nc.vector.scalar_tensor_tensor(
                    out=Ab[1][:, 0:N], in0=Ub[1][:, k:k + N], scalar=pcol(l, P_MW + f * 3 + k), in1=Ab[1][:, 0:N],
                    op0=ALU.mult, op1=ALU.add))
            sc.op(dve, [Ub[0]], [b_sup[l][f]], lambda: nc.vector.tensor_copy(out=st_up[:, l, f, :], in_=Ub[1][:, N:N + 2]))
            sc.op(act, [Ab[0]], [Ab[0]], lambda: nc.scalar.activation(
                out=Ab[1][:, 0:N], in_=Ab[1][:, 0:N], func=AF.Gelu_apprx_tanh))
            ab_, at_ = yyb[aset * 4 + fb]
            sc.op(dve, [Ab[0], bGt[0]], [ab_], lambda: nc.vector.tensor_tensor(
                out=at_[:, :], in0=Ab[1][:, 0:N], in1=bGt[1], op=ALU.mult))

    def mlp_down(l, c, t, s_, aset):
        acts = [yyb[aset * 4 + fb] for fb in range(4)]
        for oc in range(KC):
            bank = next_bank()
            sc.mm(pe, [b_md[s_]] + [a[0] for a in acts], bank,
                  [(mdn_sb[s_][:, fb * D + oc * 128:fb * D + (oc + 1) * 128], acts[fb][1][:, :]) for fb in range(4)], nc)
            sc.op(dve, [bank[0], hb[oc][t]], [hb[oc][t]], lambda oc=oc, bank=bank: nc.vector.tensor_tensor(
                out=h_sb[:, oc, tsl(t)], in0=h_sb[:, oc, tsl(t)], in1=bank[1], op=ALU.add))

    def final_tile(seq, half, t):
        srcs = [(hb[k][t], h_sb[:, k, tsl(t)]) for k in range(KC)]
        rsb, rst = norm_stats(srcs, 1.0 / D)
        for k in range(KC):
            ob, ot = STAGE[k % 6]
            sc.op(dve, [hb[k][t], rsb, b_pv], [ob], lambda k=k, ot=ot: nc.vector.scalar_tensor_tensor(
                out=ot[:, 0:N], in0=h_sb[:, k, tsl(t)], scalar=pcol(0, P_FG + k), in1=rst[:, 0:N],
                op0=ALU.mult, op1=ALU.mult))
            sc.dma(sp, c_out, [ob], [], lambda k=k, ot=ot: nc.sync.dma_start(
                out=yT[seq, k * 128:(k + 1) * 128, half * G + t * N: half * G + (t + 1) * N], in_=ot[:, 0:N]))

    for lgi, (g, l) in enumerate(lgs):
        seq, half = divmod(g, S // G)
        last_layer = (l == nlayers - 1)
        if l == 0:
            for k in range(KC):
                sc.dma(sp, c_x, [], [hb[k][t] for t in range(NT)], lambda k=k: nc.sync.dma_start(
                    out=h_sb[:, k, :], in_=xT[seq, k * 128:(k + 1) * 128, half * G:(half + 1) * G]))
            if half == 0:
                for ll in range(nlayers):
                    sc.op(dve, [], b_slx[ll], lambda ll=ll: nc.vector.memset(st_lx[:, ll, :, :], 0.0))
                    sc.op(dve, [], b_sh[ll], lambda ll=ll: nc.vector.memset(st_h[:, ll, :], 0.0))
                    sc.op(dve, [], b_scv[ll], lambda ll=ll: nc.vector.memset(st_cv[:, ll, :, :], 0.0))
                    sc.op(dve, [], b_sup[ll], lambda ll=ll: nc.vector.memset(st_up[:, ll, :, :], 0.0))
            norm_h(l, 0, P_G1)
            norm_h(l, 1, P_G1)
        phase_a(l, lgi)
        units = [(c, t) for c in range(6) for t in range(NT)]
        prev = None
        for ui, (c, t) in enumerate(units):
            s_ = c % 2
            aset = ui % 2
            mlp_up(l, c, t, s_, aset)
            if t == NT - 1:
                _prefetch_mlp(sc, lgs, lgi, c, s_, load_mlp, 0)
            if ui == 0:
                norm_h(l, 1, P_G2)
            if prev is not None:
                pc, pt, ps_, pa = prev
                mlp_down(l, pc, pt, ps_, pa)
                if pt == NT - 1:
                    _prefetch_mlp(sc, lgs, lgi, pc, ps_, load_mlp, 1)
            prev = (c, t, s_, aset)
        pc, pt, ps_, pa = prev
        if last_layer:
            final_tile(seq, half, 0)
        else:
            norm_h(l + 1, 0, P_G1)
        mlp_down(l, pc, pt, ps_, pa)
        _prefetch_mlp(sc, lgs, lgi, pc, ps_, load_mlp, 1)
        if last_layer:
            final_tile(seq, half, 1)
        else:
            norm_h(l + 1, 1, P_G1)
    nc.sync.wait_ge(c_out.sem, c_out.count)
    es.close()
    return nc


def _prefetch_mlp(sc, lgs, lgi, c, s_, load_mlp, which):
    nc_ = c + 2
    if nc_ < 6:
        load_mlp(lgs[lgi][1], nc_, s_, which)
    elif lgi + 1 < len(lgs):
        load_mlp(lgs[lgi + 1][1], nc_ - 6, s_, which)


def pack_params(inp):
    pv = np.zeros((128, L, NPV), np.float32)

    def ch(v, n):
        return np.ascontiguousarray(v.reshape(n, 128).T)
    for l in range(L):
        pv[:, l, P_G1:P_G1 + 8] = ch(inp["norm1_g"][l], 8)
        cw = inp["lru_conv_w"][l]
        for j in range(4):
            for k in range(4):
                pv[:, l, P_CW + j * 4 + k] = cw[k, j * 128:(j + 1) * 128]
        pv[:, l, P_CB:P_CB + 4] = ch(inp["lru_conv_b"][l], 4)
        pv[:, l, P_BA:P_BA + 4] = ch(inp["lru_b_a"][l].reshape(512), 4)
        pv[:, l, P_BX:P_BX + 4] = ch(inp["lru_b_x"][l].reshape(512), 4)
        pv[:, l, P_LAM:P_LAM + 4] = ch(inp["lru_lambda"][l], 4)
        sw = inp["sc_conv_w"][l]
        for j in range(4):
            for k in range(3):
                pv[:, l, P_SW + j * 3 + k] = sw[k, j * 128:(j + 1) * 128]
        pv[:, l, P_GA:P_GA + 4] = ch(inp["lru_out_g"][l], 4)
        pv[:, l, P_GB:P_GB + 4] = ch(inp["sc_out_g"][l], 4)
        pv[:, l, P_G2:P_G2 + 8] = ch(inp["norm2_g"][l], 8)
        mw = inp["mlp_conv_w"][l]
        for f in range(24):
            for k in range(3):
                pv[:, l, P_MW + f * 3 + k] = mw[k, f * 128:(f + 1) * 128]
        pv[:, l, P_MB:P_MB + 24] = ch(inp["mlp_conv_b"][l], 24)
        pv[:, l, P_FG:P_FG + 8] = ch(inp["final_g"], 8)
    return pv


def permute_w_in(w):
    out = np.empty_like(w)
    for j in range(4):
        for bi, off in enumerate(BLK_OFFS):
            out[:, :, j * 640 + bi * 128: j * 640 + (bi + 1) * 128] = w[:, :, off + j * 128: off + (j + 1) * 128]
    return out


def relayout_weights(inp):
    f = lambda a: np.asarray(a, dtype=np.float32)
    w_in_p = permute_w_in(f(inp["w_in"]))
    return {
        "w_in": np.ascontiguousarray(w_in_p.reshape(L, 8, 128, 4, 640).transpose(0, 3, 2, 1, 4)).reshape(L, 4, 128, 5120),
        "w_out": np.ascontiguousarray(f(inp["w_out"]).reshape(L, 8, 128, 1024).transpose(0, 2, 1, 3)).reshape(L, 128, 8192),
        "mlp_w_up": np.ascontiguousarray(f(inp["mlp_w_up"]).reshape(L, 8, 128, 6, 512).transpose(0, 3, 2, 1, 4)).reshape(L, 6, 128, 4096),
        "mlp_w_gate": np.ascontiguousarray(f(inp["mlp_w_gate"]).reshape(L, 8, 128, 6, 512).transpose(0, 3, 2, 1, 4)).reshape(L, 6, 128, 4096),
        "mlp_w_down": np.ascontiguousarray(f(inp["mlp_w_down"]).reshape(L, 6, 4, 128, 1024).transpose(0, 1, 3, 2, 4)).reshape(L, 6, 128, 4096),
        "lru_w_a": np.ascontiguousarray(f(inp["lru_w_a"])),
        "lru_w_x": np.ascontiguousarray(f(inp["lru_w_x"])),
    }


_NC_CACHE = {}


def kernel(**inp):
    inp = {k: np.asarray(v) for k, v in inp.items()}
    x = inp["x"].astype(np.float32, copy=False)
    if "full" not in _NC_CACHE:
        _NC_CACHE["full"] = build_nc()
    nc = _NC_CACHE["full"]
    pv = pack_params(inp)
    shared = relayout_weights(inp)
    shared["pvec"] = pv
    in_maps = []
    for c in range(NCORES):
        xs = x[c * SPC:(c + 1) * SPC]
        m = dict(shared)
        m["xT"] = np.ascontiguousarray(xs.transpose(0, 2, 1))
        in_maps.append(m)
    res = run_bass_kernel_spmd(nc, in_maps, core_ids=list(range(NCORES)))
    out = np.empty((NCORES * SPC, S, D), np.float32)
    for c in range(NCORES):
        out[c * SPC:(c + 1) * SPC] = res.results[c]["yT"].transpose(0, 2, 1)
    return out
```
